# Optimizing a Trainium2 kernel written in Bass

```python
import jax
import jax.numpy as jnp
from jax import lax
import numpy as np

D_MODEL = 2048
BATCH = 4
SEQ = 2048
DEPTH = 2

MLSTM_HEADS = 4
MLSTM_V_DIM = D_MODEL // (2 * MLSTM_HEADS)
MLSTM_QK_DIM = MLSTM_V_DIM // 2
MLSTM_CHUNK = 64
CONV_WIDTH = 4
HEAD_DIM = 64
ATTN_HEADS = D_MODEL // (2 * HEAD_DIM)
KV_HEADS = 4
WINDOW = 128
ATTN_BLOCK = WINDOW
D_FF = 4 * D_MODEL
NORM_EPS = 1e-6
N_MOD = 6

MLSTM_QK_W = MLSTM_HEADS * MLSTM_QK_DIM
MLSTM_V_W = MLSTM_HEADS * MLSTM_V_DIM
ATTN_Q_W = ATTN_HEADS * HEAD_DIM
ATTN_KV_W = KV_HEADS * HEAD_DIM
D_MIX = MLSTM_V_W + ATTN_Q_W
IN_SIZES = (MLSTM_QK_W, MLSTM_QK_W, MLSTM_V_W, MLSTM_V_W, MLSTM_HEADS, MLSTM_HEADS, ATTN_Q_W, ATTN_KV_W, ATTN_KV_W)
D_IN = sum(IN_SIZES)

kernel_name = 'hymba_mlstm_swa_sink_alibi_sandwich_adaln'


def rms_norm(x, g):
    xf = x.astype(jnp.float32)
    y = xf * lax.rsqrt(jnp.mean(xf * xf, axis=-1, keepdims=True) + NORM_EPS)
    return (y * g.astype(jnp.float32)).astype(x.dtype)


def causal_depthwise_conv(x, w, b):
    y = lax.conv_general_dilated(
        x, w[:, None, :].astype(x.dtype), window_strides=(1,),
        padding=((CONV_WIDTH - 1, 0),), dimension_numbers=('NWC', 'WIO', 'NWC'),
        feature_group_count=x.shape[-1])
    return y + b.astype(x.dtype)


def mlstm_chunkwise(q, k, v, i_pre, f_pre):
    B, S, H, Dk = q.shape
    Dv = v.shape[-1]
    L = MLSTM_CHUNK
    NC = S // L
    q = q.reshape(B, NC, L, H, Dk) * (Dk ** -0.5)
    k = k.reshape(B, NC, L, H, Dk)
    v = v.reshape(B, NC, L, H, Dv)
    ig = i_pre.reshape(B, NC, L, H)
    b = jnp.cumsum(jax.nn.log_sigmoid(f_pre).reshape(B, NC, L, H), axis=2)
    b_end = b[:, :, -1]

    a = b_end[:, :, None] - b + ig
    m_loc = jnp.max(a, axis=2)
    kw = k * jnp.exp(a - m_loc[:, :, None])[..., None]
    C_loc = jnp.einsum('bclhk,bclhv->bchkv', kw, v)
    n_loc = jnp.sum(kw, axis=2)

    def step(carry, inp):
        C, n, m = carry
        be, ml, Cl, nl = inp
        m_new = jnp.maximum(be + m, ml)
        s_prev = jnp.exp(be + m - m_new)
        s_loc = jnp.exp(ml - m_new)
        C_new = s_prev[..., None, None] * C + s_loc[..., None, None] * Cl
        n_new = s_prev[..., None] * n + s_loc[..., None] * nl
        return (C_new, n_new, m_new), (C, n, m)

    init = (jnp.zeros((B, H, Dk, Dv), q.dtype), jnp.zeros((B, H, Dk), q.dtype), jnp.zeros((B, H), q.dtype))
    xs = tuple(jnp.moveaxis(t, 1, 0) for t in (b_end, m_loc, C_loc, n_loc))
    _, (C_prev, n_prev, m_prev) = lax.scan(step, init, xs)
    C_prev = jnp.moveaxis(C_prev, 0, 1)
    n_prev = jnp.moveaxis(n_prev, 0, 1)
    m_prev = jnp.moveaxis(m_prev, 0, 1)

    causal = jnp.tril(jnp.ones((L, L), dtype=bool))
    Dlog = b[:, :, :, None, :] - b[:, :, None, :, :] + ig[:, :, None, :, :]
    Dlog = jnp.where(causal[None, None, :, :, None], Dlog, -jnp.inf)
    inter = b + m_prev[:, :, None, :]
    m_t = jnp.maximum(inter, jnp.max(Dlog, axis=3))
    Pqk = jnp.exp(Dlog - m_t[:, :, :, None, :]) * jnp.einsum('bcthd,bcshd->bctsh', q, k)
    s_inter = jnp.exp(inter - m_t)
    num = (jnp.einsum('bctsh,bcshv->bcthv', Pqk, v)
           + s_inter[..., None] * jnp.einsum('bcthd,bchdv->bcthv', q, C_prev))
    den = jnp.sum(Pqk, axis=3) + s_inter * jnp.einsum('bcthd,bchd->bcth', q, n_prev)
    h = num / jnp.maximum(jnp.abs(den), jnp.exp(-m_t))[..., None]
    return h.reshape(B, S, H, Dv)


def sliding_window_sink_attention(q, k, v, sinks):
    B, S, Hq, Dh = q.shape
    Hkv = k.shape[2]
    G = Hq // Hkv
    T = ATTN_BLOCK
    NB = S // T
    f32 = jnp.float32
    qb = q.astype(f32).reshape(B, NB, T, Hkv, G, Dh) * (Dh ** -0.5)

    def banded(t):
        tb = t.astype(f32).reshape(B, NB, T, Hkv, Dh)
        prev = jnp.pad(tb, ((0, 0), (1, 0), (0, 0), (0, 0), (0, 0)))[:, :-1]
        return jnp.concatenate([prev, tb], axis=2)

    kb, vb = banded(k), banded(v)
    scores = jnp.einsum('bnqhgd,bnkhd->bnhgqk', qb, kb)
    r = jnp.arange(T)[:, None]
    u = jnp.arange(2 * T)[None, :]
    dist = r + T - u
    key_pos = jnp.arange(NB)[:, None] * T + jnp.arange(2 * T)[None, :] - T
    valid = ((dist >= 0) & (dist < WINDOW))[None] & (key_pos >= 0)[:, None, :]
    slopes = jnp.exp2(-8.0 * (jnp.arange(Hq, dtype=f32) + 1.0) / Hq).reshape(Hkv, G)
    alibi = -slopes[:, :, None, None] * dist.astype(f32)
    scores = jnp.where(valid[None, :, None, None], scores + alibi[None, None], -jnp.inf)
    sink = sinks.astype(f32).reshape(Hkv, G)[None, None, :, :, None, None]
    m = jnp.maximum(jnp.max(scores, axis=-1, keepdims=True), sink)
    p = jnp.exp(scores - m)
    probs = p / (jnp.sum(p, axis=-1, keepdims=True) + jnp.exp(sink - m))
    out = jnp.einsum('bnhgqk,bnkhd->bnqhgd', probs, vb)
    return out.reshape(B, S, Hq, Dh)


def token_mixer(h, w_in, conv_w, conv_b, b_i, b_f, g_mlstm_head, g_attn_out, attn_sinks, w_out):
    B, S, _ = h.shape
    f32 = jnp.float32
    proj = h @ w_in
    offs = [int(o) for o in np.cumsum(IN_SIZES)[:-1]]
    q_m, k_m, v_m, o_m, i_pre, f_pre, q_a, k_a, v_a = jnp.split(proj, offs, axis=-1)

    qk = jax.nn.silu(causal_depthwise_conv(jnp.concatenate([q_m, k_m], axis=-1), conv_w, conv_b))
    q_m, k_m = jnp.split(qk, 2, axis=-1)
    hm = mlstm_chunkwise(
        q_m.reshape(B, S, MLSTM_HEADS, MLSTM_QK_DIM).astype(f32),
        k_m.reshape(B, S, MLSTM_HEADS, MLSTM_QK_DIM).astype(f32),
        v_m.reshape(B, S, MLSTM_HEADS, MLSTM_V_DIM).astype(f32),
        (i_pre + b_i).astype(f32), (f_pre + b_f).astype(f32))
    hm = rms_norm(hm, g_mlstm_head) * jax.nn.sigmoid(o_m.reshape(B, S, MLSTM_HEADS, MLSTM_V_DIM).astype(f32))
    hm = hm.reshape(B, S, MLSTM_V_W).astype(h.dtype)

    ha = sliding_window_sink_attention(
        q_a.reshape(B, S, ATTN_HEADS, HEAD_DIM),
        k_a.reshape(B, S, KV_HEADS, HEAD_DIM),
        v_a.reshape(B, S, KV_HEADS, HEAD_DIM), attn_sinks)
    ha = rms_norm(ha.reshape(B, S, ATTN_Q_W), g_attn_out).astype(h.dtype)

    return jnp.concatenate([hm, ha], axis=-1) @ w_out


def squared_relu_mlp(h, w_up, w_down):
    return jnp.square(jax.nn.relu(h @ w_up)) @ w_down


def setup_inputs(seed: int = 0) -> dict:
    key = jax.random.key(seed)
    ks = jax.random.split(key, 24)
    nrm = jax.random.normal
    L = DEPTH
    def gain(k, shape):
        return 1.0 + 0.05 * nrm(k, shape, jnp.float32)
    return {
        'x': nrm(ks[0], (BATCH, SEQ, D_MODEL), jnp.float32),
        'c': nrm(ks[1], (BATCH, D_MODEL), jnp.float32),
        'w_ada': 0.5 * D_MODEL ** -0.5 * nrm(ks[2], (L, D_MODEL, N_MOD * D_MODEL), jnp.float32),
        'b_ada': 0.02 * nrm(ks[3], (L, N_MOD * D_MODEL), jnp.float32),
        'g_pre_mix': gain(ks[4], (L, D_MODEL)),
        'g_post_mix': gain(ks[5], (L, D_MODEL)),
        'g_pre_mlp': gain(ks[6], (L, D_MODEL)),
        'g_post_mlp': gain(ks[7], (L, D_MODEL)),
        'w_in': D_MODEL ** -0.5 * nrm(ks[8], (L, D_MODEL, D_IN), jnp.float32),
        'conv_w': CONV_WIDTH ** -0.5 * nrm(ks[9], (L, CONV_WIDTH, 2 * MLSTM_QK_W), jnp.float32),
        'conv_b': 0.02 * nrm(ks[10], (L, 2 * MLSTM_QK_W), jnp.float32),
        'b_i': 0.1 * nrm(ks[11], (L, MLSTM_HEADS), jnp.float32),
        'b_f': jnp.linspace(3.0, 6.0, MLSTM_HEADS, dtype=jnp.float32)[None, :] + 0.1 * nrm(ks[12], (L, MLSTM_HEADS), jnp.float32),
        'g_mlstm_head': gain(ks[13], (L, MLSTM_HEADS, MLSTM_V_DIM)),
        'g_attn_out': gain(ks[14], (L, ATTN_Q_W)),
        'attn_sinks': 0.5 * nrm(ks[15], (L, ATTN_HEADS), jnp.float32),
        'w_out': D_MIX ** -0.5 * nrm(ks[16], (L, D_MIX, D_MODEL), jnp.float32),
        'w_up': D_MODEL ** -0.5 * nrm(ks[17], (L, D_MODEL, D_FF), jnp.float32),
        'w_down': D_FF ** -0.5 * nrm(ks[18], (L, D_FF, D_MODEL), jnp.float32),
    }


def reference(x, c, w_ada, b_ada, g_pre_mix, g_post_mix, g_pre_mlp, g_post_mlp, w_in, conv_w, conv_b,
              b_i, b_f, g_mlstm_head, g_attn_out, attn_sinks, w_out, w_up, w_down):
    c_act = jax.nn.silu(c)
    for l in range(DEPTH):
        mod = c_act @ w_ada[l] + b_ada[l]
        shift_a, scale_a, gate_a, shift_m, scale_m, gate_m = [m[:, None, :] for m in jnp.split(mod, N_MOD, axis=-1)]
        h = rms_norm(x, g_pre_mix[l]) * (1.0 + scale_a) + shift_a
        y = token_mixer(h, w_in[l], conv_w[l], conv_b[l], b_i[l], b_f[l], g_mlstm_head[l],
                        g_attn_out[l], attn_sinks[l], w_out[l])
        x = x + gate_a * rms_norm(y, g_post_mix[l])
        h = rms_norm(x, g_pre_mlp[l]) * (1.0 + scale_m) + shift_m
        y = squared_relu_mlp(h, w_up[l], w_down[l])
        x = x + gate_m * rms_norm(y, g_post_mlp[l])
    return x
```

```python
from contextlib import ExitStack
import math
import numpy as np
import concourse.bass as bass
import concourse.mybir as mybir
from concourse.bass_utils import run_bass_kernel_spmd

F32 = mybir.dt.float32
BF16 = mybir.dt.bfloat16
AF = mybir.ActivationFunctionType
ALU = mybir.AluOpType
ENG = ["pe", "act", "dve", "pool", "sp"]
NDMASEM = 24

D = 2048
KC = 16
TOK = 1024
DIN = 4616
DFF = 8192
EPS = 1e-6
NW = 5
NVEC = 136
NCONST = 640
SB_TOP = 229344


class _Rec:
    def __init__(self):
        self.calls = []

    def __getattr__(self, nm):
        def f(*a, **kw):
            self.calls.append((nm, a, kw))
        return f


class Prog:
    def __init__(self, nc):
        self.nc = nc
        self.ops = {e: [] for e in ENG}
        self.cnt = {e: 0 for e in ENG}
        self.seen = {e: {} for e in ENG}
        self.lastw = {}
        self.readers = {}
        self.dma_cnt = [0] * NDMASEM
        self.dma_rr = 0
        self.dma_rr_e = {}
        self.sb_off = 16640
        self.bank_rr = 0
        self.nalloc = 0

    def sb(self, name, shape, dtype, off=None):
        esz = 4 if dtype == F32 else 2
        n = 1
        for s in shape[1:]:
            n *= s
        nbytes = n * esz
        if off is None:
            off = (self.sb_off + 63) // 64 * 64
            self.sb_off = off + nbytes
        assert off + nbytes <= SB_TOP, (name, off, nbytes)
        self.nalloc += 1
        return self.nc.alloc_sbuf_tensor_at(f"{name}_{self.nalloc}", list(shape), dtype, offset=off)

    def op(self, eng, fns, reads=(), writes=(), dma=False):
        waits = {}

        def need(tok):
            if tok is None:
                return
            k, v = tok
            if waits.get(k, 0) < v:
                waits[k] = v

        for key in reads:
            need(self.lastw.get(key))
        for key in writes:
            need(self.lastw.get(key))
            for t in self.readers.get(key, ()):
                need(t)
        wl = []
        for k, v in waits.items():
            if self.seen[eng].get(k, 0) >= v:
                continue
            self.seen[eng][k] = v
            wl.append((k, v))
        if not isinstance(fns, list):
            fns = [fns]
        rec = _Rec()
        for f in fns:
            f(rec)
        fns = rec.calls
        if len(fns) == 0:
            tok = None
        elif dma:
            lo, n = (0, 8) if eng == "pool" else (8, NDMASEM - 8)
            r = self.dma_rr_e.get(eng, 0)
            self.dma_rr_e[eng] = (r + 1) % n
            s = lo + r
            kprev = ("dma", s)
            if self.dma_cnt[s] > 0 and self.seen[eng].get(kprev, 0) < self.dma_cnt[s]:
                self.seen[eng][kprev] = self.dma_cnt[s]
                wl.append((kprev, self.dma_cnt[s]))
            self.dma_cnt[s] += 16
            tok = (("dma", s), self.dma_cnt[s])
        else:
            self.cnt[eng] += 1
            tok = (eng, self.cnt[eng])
        self.ops[eng].append((wl, fns, tok))
        if tok is not None:
            for key in reads:
                self.readers.setdefault(key, []).append(tok)
            for key in writes:
                self.lastw[key] = tok
                self.readers[key] = []
        return tok

    def barrier(self, engines=("pe", "act", "dve", "sp")):
        for e in engines:
            wl = []
            for k in ENG:
                v = self.cnt[k]
                if v > 0 and self.seen[e].get(k, 0) < v:
                    self.seen[e][k] = v
                    wl.append((k, v))
            for s in range(NDMASEM):
                v = self.dma_cnt[s]
                k = ("dma", s)
                if v > 0 and self.seen[e].get(k, 0) < v:
                    self.seen[e][k] = v
                    wl.append((k, v))
            self.ops[e].append((wl, [], None))

    def emit(self):
        nc = self.nc
        with ExitStack() as st:
            sems = {e: st.enter_context(nc.semaphore(f"s_{e}")) for e in ENG}
            dsems = [st.enter_context(nc.semaphore(f"d{i}")) for i in range(NDMASEM)]
            block = st.enter_context(nc.Block())

            def run(name, e):
                for wl, fns, tok in self.ops[name]:
                    for k, v in wl:
                        sem = sems[k] if isinstance(k, str) else dsems[k[1]]
                        e.wait_ge(sem, v)
                    ins = None
                    for (nm, a, kw) in fns:
                        ins = getattr(e, nm)(*a, **kw)
                    if tok is not None:
                        k, v = tok
                        if isinstance(k, str):
                            ins.then_inc(sems[k], 1)
                        else:
                            ins.then_inc(dsems[k[1]], 16)

            @block.tensor
            def _(e):
                run("pe", e)

            @block.scalar
            def _(e):
                run("act", e)

            @block.vector
            def _(e):
                run("dve", e)

            @block.gpsimd
            def _(e):
                run("pool", e)

            @block.sync
            def _(e):
                run("sp", e)


def attn_head_of_slot(c, hf):
    return (c if c < 4 else 8 + (c - 4)) + 4 * hf


def build_program(layers, dbg=False):
    nc = bass.Bass("TRN2", target_bir_lowering=False)
    P = Prog(nc)

    def din(name, shape):
        return nc.dram_tensor(name, list(shape), F32, kind="ExternalInput").ap()

    def dout(name, shape):
        return nc.dram_tensor(name, list(shape), F32, kind="ExternalOutput").ap()

    xT1_d = din("xT1", [D, TOK])
    xT2_d = din("xT2", [D, TOK])
    cT_d = din("cT", [128, KC])
    consts_d = din("consts", [128, NCONST])
    stval_d = din("st_valid", [128, 1])
    W = []
    for i in range(len(layers)):
        W.append(dict(
            w_ada=din(f"w_ada{i}", [D, 6 * D]), b_ada=din(f"b_ada{i}", [128, 96]),
            w_in=din(f"w_in{i}", [D, DIN]), w_out=din(f"w_out{i}", [D, D]),
            w_up=din(f"w_up{i}", [D, DFF]), w_down=din(f"w_down{i}", [DFF, D]),
            vecs=din(f"vecs{i}", [128, NVEC])))
    outT_d = dout("outT", [D, TOK])
    if dbg:
        dbg_mo_d = dout("dbg_mo", [128, 16, TOK])
        dbg_x_d = dout("dbg_x", [D, TOK])
        dbg_h_d = dout("dbg_h", [D, TOK])
        dbg_mod_d = dout("dbg_mod", [128, 96])

    ps = [nc.alloc_psum_tensor(f"ps{i}", [128, 512], F32) for i in range(8)]

    reserved = set()

    def bank():
        while True:
            b = P.bank_rr
            P.bank_rr = (b + 1) % 8
            if b not in reserved:
                return b

    xT = P.sb("xT", [128, KC, TOK], F32)
    consts = P.sb("consts", [128, NCONST], F32)
    ident_f = consts[:, 0:128]
    mask01 = consts[:, 128:256]
    distp = consts[:, 256:384]
    distc = consts[:, 384:512]
    utri = consts[:, 512:640]
    ident_b = P.sb("ident_b", [128, 128], BF16)
    ones_b = P.sb("ones_b", [128, 128], BF16)
    ones_f = P.sb("ones_f", [128, 128], F32)
    vecs = P.sb("vecs", [128, NVEC], F32)
    mod = P.sb("mod", [128, len(layers), 96], F32)
    bada = P.sb("bada", [128, 96], F32)
    der = P.sb("der", [128, 4, KC], F32)
    cact = P.sb("cact", [128, KC], BF16)
    cT = P.sb("cT", [128, KC], F32)
    sinkexp = P.sb("sinkexp", [128, 8], F32)
    epst = P.sb("epst", [128, 1], F32)
    lnsct = P.sb("lnsct", [128, 1], F32)
    stval = P.sb("stval", [128, 1], F32)
    S_C = [nc.dram_tensor(f"S_C{l}", [128, 4, 257], F32).ap() for l in range(2)]
    S_conv = [nc.dram_tensor(f"S_conv{l}", [128, 8, 3], F32).ap() for l in range(2)]
    S_k = [nc.dram_tensor(f"S_k{l}", [128, 2, 128], F32).ap() for l in range(2)]
    S_v = [nc.dram_tensor(f"S_v{l}", [128, 256], F32).ap() for l in range(2)]
    wr = [P.sb(f"wr{s}", [128, KC, 128], BF16) for s in range(NW)]
    sqr = [P.sb(f"sq{s}", [128, 512], BF16) for s in range(3)]
    rt = P.sb("rt", [128, 512], F32)
    rstd = P.sb("rstd", [128, 512], F32)
    tmpf = [P.sb(f"tmpf{s}", [128, 512], F32) for s in range(2)]
    R0 = (P.sb_off + 63) // 64 * 64
    st = {"wr": 0, "sq": 0, "tmp": 0}

    def rot(name, n):
        i = st[name]
        st[name] = (i + 1) % n
        return i

    mo = P.sb("mo", [128, 16, TOK], BF16, off=R0)
    hT = P.sb("hT", [128, KC, TOK], BF16, off=R0 + 32768)
    RT = R0 + 65536
    ystash = P.sb("ystash", [128, KC, TOK], F32, off=R0 + 32768)
    actT = P.sb("actT", [128, 64, 512], BF16, off=R0)
    y32 = P.sb("y32", [128, KC, 512], F32, off=R0 + 65536)
    hpT = P.sb("hpT", [128, KC, 512], BF16, off=R0 + 65536)
    assert R0 + 65536 + 40960 <= SB_TOP, R0

    def wload(wap, k0, c0, ncols):
        s = rot("wr", NW)
        src = wap[k0 * 128:(k0 + KC) * 128, c0:c0 + ncols].rearrange("(k p) n -> p k n", p=128)
        P.op("pool", lambda e, s=s: e.dma_start(out=wr[s][:, :, 0:ncols], in_=src), writes=[f"wr{s}"], dma=True)
        return s

    def rms_stats(src_fn, nchunks, width, srckeys, scale):
        b = bank()
        pend = None
        for k in range(nchunks):
            s = rot("sq", 3)
            P.op("act", lambda e, s=s, k=k: e.activation(out=sqr[s][:, 0:width], in_=src_fn(k), func=AF.Square),
                 reads=srckeys(k), writes=[f"sq{s}"])
            if pend is not None:
                pk, ps_ = pend
                P.op("pe", lambda e, ps_=ps_, pk=pk: e.matmul(ps[b][:, 0:width], ones_b[:], sqr[ps_][:, 0:width],
                                                              start=(pk == 0), stop=False),
                     reads=[f"sq{ps_}"], writes=[f"ps{b}"])
            pend = (k, s)
        pk, ps_ = pend
        P.op("pe", lambda e: e.matmul(ps[b][:, 0:width], ones_b[:], sqr[ps_][:, 0:width], start=(pk == 0), stop=True),
             reads=[f"sq{ps_}"], writes=[f"ps{b}"])
        P.op("act", lambda e: e.activation(out=rt[:, 0:width], in_=ps[b][:, 0:width], func=AF.Sqrt, scale=scale, bias=epst[:, 0:1]),
             reads=[f"ps{b}"], writes=["rt"])
        P.op("dve", lambda e: e.reciprocal(out=rstd[:, 0:width], in_=rt[:, 0:width]), reads=["rt"], writes=["rstd"])

    def prenorm(dst, dstkey, t0, width, gsi, shi_ap):
        rms_stats(lambda k: xT[:, k, t0:t0 + width], KC, width, lambda k: [f"x{k}"], 1.0 / D)
        for k in range(KC):
            i = rot("tmp", 2)
            P.op("dve", lambda e, k=k, i=i: e.scalar_tensor_tensor(out=tmpf[i][:, 0:width], in0=xT[:, k, t0:t0 + width],
                                                                   scalar=der[:, gsi, k:k + 1], in1=rstd[:, 0:width],
                                                                   op0=ALU.mult, op1=ALU.mult),
                 reads=[f"x{k}", "rstd", "der"], writes=[f"tmpf{i}"])
            P.op("act", lambda e, k=k, i=i: e.activation(out=dst(k), in_=tmpf[i][:, 0:width], func=AF.Identity,
                                                         bias=shi_ap(k), scale=1.0),
                 reads=[f"tmpf{i}", "mod"], writes=[dstkey(k)])

    def proj_fm(wap, c0, nchunk, rhs_fn, rhskeys, ntt, evac, k0=0, nk=KC):
        for c in range(nchunk):
            s = wload(wap, k0, c0 + c * 128, 128)
            for tt in range(ntt):
                b = bank()
                fns = [(lambda e, k=k: e.matmul(ps[b][:], wr[s][:, k, :], rhs_fn(k, tt), start=(k == 0), stop=(k == nk - 1)))
                       for k in range(nk)]
                P.op("pe", fns, reads=[f"wr{s}"] + rhskeys(tt), writes=[f"ps{b}"])
                evac(c, tt, b)

    P.op("sp", lambda e: e.dma_start(out=consts[:], in_=consts_d), writes=["consts"], dma=True)
    P.op("sp", lambda e: e.dma_start(out=cT[:], in_=cT_d), writes=["cT"], dma=True)
    P.op("sp", lambda e: e.dma_start(out=stval[:], in_=stval_d), writes=["stval"], dma=True)
    def load_x(src):
        for k in range(KC):
            P.op("sp", lambda e, k=k: e.dma_start(out=xT[:, k, :], in_=src[k * 128:(k + 1) * 128, :]), writes=[f"x{k}"], dma=True)
    load_x(xT1_d)
    P.op("dve", lambda e: e.memset(ones_f[:], 1.0), writes=["ones_f"])
    P.op("dve", lambda e: e.memset(ones_b[:], 1.0), writes=["ones_b"])
    P.op("dve", lambda e: e.memset(epst[:], EPS), writes=["epst"])
    P.op("dve", lambda e: e.memset(lnsct[:], math.log(128.0 ** -0.5)), writes=["lnsct"])
    P.op("dve", lambda e: e.tensor_copy(out=ident_b[:], in_=ident_f), reads=["consts"], writes=["ident_b"])
    P.op("act", lambda e: e.activation(out=cact[:], in_=cT[:], func=AF.Silu), reads=["cT"], writes=["cact"])

    for li in range(len(layers)):
        P.op("sp", lambda e, li=li: e.dma_start(out=bada[:], in_=W[li]["b_ada"]), writes=["bada"], dma=True)
        b = bank()
        for j in range(96):
            s = wload(W[li]["w_ada"], 0, j * 128, 128)
            fns = [(lambda e, k=k, s=s, j=j: e.matmul(ps[b][:, j:j + 1], wr[s][:, k, :], cact[:, k:k + 1],
                                                    start=(k == 0), stop=(k == KC - 1))) for k in range(KC)]
            P.op("pe", fns, reads=[f"wr{s}", "cact"], writes=[f"ps{b}"])
        P.op("dve", lambda e, li=li, b=b: e.tensor_tensor(out=mod[:, li, :], in0=ps[b][:, 0:96], in1=bada[:], op=ALU.add),
             reads=[f"ps{b}", "bada"], writes=["mod"])

    def emit_unit(li, full, init, save):
        Wl = W[li]
        P.op("sp", lambda e, li=li: e.dma_start(out=vecs[:], in_=W[li]["vecs"]), writes=["vecs"], dma=True)
        g_pre_mix, g_post_mix, g_pre_mlp, g_post_mlp = (vecs[:, 0:16], vecs[:, 16:32], vecs[:, 32:48], vecs[:, 48:64])
        convw = vecs[:, 64:96]
        convb = vecs[:, 96:104]
        g_ml = vecs[:, 104:112]
        g_at = vecs[:, 112:120]
        sinks2 = vecs[:, 120:128]
        gbias = vecs[:, 128:136]
        shift_a, scale_a, gate_a = mod[:, li, 0:16], mod[:, li, 16:32], mod[:, li, 32:48]
        shift_m, scale_m, gate_m = mod[:, li, 48:64], mod[:, li, 64:80], mod[:, li, 80:96]
        P.op("dve", lambda e: e.scalar_tensor_tensor(out=der[:, 0, :], in0=scale_a, scalar=1.0, in1=g_pre_mix, op0=ALU.add, op1=ALU.mult),
             reads=["mod", "vecs"], writes=["der"])
        P.op("dve", lambda e: e.tensor_tensor(out=der[:, 1, :], in0=gate_a, in1=g_post_mix, op=ALU.mult), reads=["mod", "vecs"], writes=["der"])
        P.op("dve", lambda e: e.scalar_tensor_tensor(out=der[:, 2, :], in0=scale_m, scalar=1.0, in1=g_pre_mlp, op0=ALU.add, op1=ALU.mult),
             reads=["mod", "vecs"], writes=["der"])
        P.op("dve", lambda e: e.tensor_tensor(out=der[:, 3, :], in0=gate_m, in1=g_post_mlp, op=ALU.mult), reads=["mod", "vecs"], writes=["der"])
        P.op("act", lambda e: e.activation(out=sinkexp[:], in_=sinks2, func=AF.Exp), reads=["vecs"], writes=["sinkexp"])

        P.barrier()
        for tt in range(2):
            prenorm(lambda k, tt=tt: hT[:, k, tt * 512:(tt + 1) * 512], lambda k, tt=tt: f"hT{tt}", tt * 512, 512, 0,
                    lambda k: shift_a[:, k:k + 1])
        hkeys = lambda tt: [f"hT{tt}"]
        if dbg and li == 0 and not init:
            P.op("sp", lambda e: e.dma_start(out=dbg_mod_d, in_=mod[:, 0, :]), reads=["mod"], writes=["out"], dma=True)
            for k in range(KC):
                for tt in range(2):
                    i = rot("tmp", 2)
                    P.op("act", lambda e, k=k, tt=tt, i=i: e.activation(out=tmpf[i][:], in_=hT[:, k, tt * 512:(tt + 1) * 512], func=AF.Copy),
                         reads=[f"hT{tt}"], writes=[f"tmpf{i}"])
                    P.op("sp", lambda e, k=k, tt=tt, i=i: e.dma_start(out=dbg_h_d[k * 128:(k + 1) * 128, tt * 512:(tt + 1) * 512], in_=tmpf[i][:]),
                         reads=[f"tmpf{i}"], writes=["out"], dma=True)
        hrhs = lambda k, tt: hT[:, k, tt * 512:(tt + 1) * 512]

        P.sb_off = RT
        q_m = P.sb("q_m", [128, 4, TOK], BF16)
        k_m = P.sb("k_m", [128, 4, TOK], BF16)
        gpre = P.sb("gpre", [128, 8, 8], F32)
        gz = P.sb("gz", [128, 8, 8], F32)
        lf = P.sb("lf", [128, 8, 4], F32)
        nb = P.sb("nb", [128, 8, 4], F32)
        app = P.sb("app", [128, 8, 4], F32)
        eka = P.sb("eka", [128, 8, 4], F32)
        ebend = P.sb("ebend", [128, 8, 4], F32)
        C32 = P.sb("C32", [128, 4, 257], F32)
        C_b = P.sb("C_b", [128, 4, 256], BF16)
        nrep = P.sb("nrep", [128, 4, 128], BF16)
        hraw = P.sb("hraw", [128, 8, 128], F32)
        sq8 = P.sb("sq8", [128, 8, 128], BF16)
        convt = P.sb("convt", [128, 8, 3], F32)
        qtail = P.sb("qtail", [128, 4, 8], F32)
        m_al = P.sb_off
        pre = [P.sb(f"pre{i}", [128, 3 + TOK], F32) for i in range(2)]
        P.sb_off = m_al
        Brep = [P.sb(f"Brep{i}", [128, 128], F32) for i in range(4)]
        Ebt = P.sb("Ebt", [128, 4, 128], F32)
        Mm = P.sb("Mm", [128, 4, 128], F32)
        PT = P.sb("PT", [128, 4, 128], BF16)
        qs = P.sb("qs", [128, 4, 128], BF16)
        kw = P.sb("kw", [128, 4, 128], BF16)
        assert P.sb_off <= RT + 40960 + 2048, P.sb_off - RT
        v_m = P.sb("v_m", [128, 8, 4, 256], BF16, off=R0 + 16384)

        if init:
            P.op("sp", lambda e: e.dma_start(out=convt[:], in_=S_conv[li]), reads=[f"Sconv{li}"], writes=["convt"], dma=True)
            P.op("sp", lambda e: e.dma_start(out=C32[:], in_=S_C[li]), reads=[f"SC{li}"], writes=["C32"], dma=True)
            P.op("dve", lambda e: e.tensor_scalar(out=convt[:], in0=convt[:], scalar1=stval[:, 0:1], scalar2=None, op0=ALU.mult),
                 reads=["convt", "stval"], writes=["convt"])
            P.op("dve", lambda e: e.tensor_scalar(out=C32[:], in0=C32[:], scalar1=stval[:, 0:1], scalar2=None, op0=ALU.mult),
                 reads=["C32", "stval"], writes=["C32"])
        else:
            P.op("dve", lambda e: e.memset(convt[:], 0.0), writes=["convt"])
            P.op("dve", lambda e: e.memset(C32[:], 0.0), writes=["C32"])

        def evac_qk(c, tt, b):
            i = c % 2
            P.op("act", lambda e: e.activation(out=pre[i][:, 3 + tt * 512:3 + (tt + 1) * 512], in_=ps[b][:], func=AF.Copy),
                 reads=[f"ps{b}"], writes=[f"pre{i}"])
            if tt == 1:
                P.op("dve", lambda e: e.tensor_copy(out=pre[i][:, 0:3], in_=convt[:, c, :]), reads=["convt"], writes=[f"pre{i}"])
                P.op("dve", lambda e: e.tensor_scalar(out=cacc[:], in0=pre[i][:, 3:3 + TOK], scalar1=convw[:, c * 4 + 3:c * 4 + 4],
                                                      scalar2=convb[:, c:c + 1], op0=ALU.mult, op1=ALU.add),
                     reads=[f"pre{i}", "vecs"], writes=["cacc"])
                for j in (2, 1, 0):
                    P.op("dve", lambda e, j=j: e.scalar_tensor_tensor(out=cacc[:], in0=pre[i][:, j:j + TOK],
                                                                      scalar=convw[:, c * 4 + j:c * 4 + j + 1], in1=cacc[:],
                                                                      op0=ALU.mult, op1=ALU.add),
                         reads=[f"pre{i}", "vecs", "cacc"], writes=["cacc"])
                dst = q_m[:, c, :] if c < 4 else k_m[:, c - 4, :]
                P.op("act", lambda e: e.activation(out=dst, in_=cacc[:], func=AF.Silu), reads=["cacc"], writes=["qk_m"])
                if save:
                    P.op("sp", lambda e: e.dma_start(out=S_conv[li][:, c, :], in_=pre[i][:, TOK:TOK + 3]), reads=[f"pre{i}"],
                         writes=[f"Sconv{li}"], dma=True)

        cacc = P.sb("cacc", [128, TOK], F32, off=R0)
        if full:
            proj_fm(Wl["w_in"], 0, 8, hrhs, hkeys, 2, evac_qk)
        else:
            for c in range(4):
                s_ = wload(Wl["w_in"], 0, c * 128, 128)
                b_ = bank()
                P.op("pe", [(lambda e, k=k: e.matmul(ps[b_][:, 0:8], wr[s_][:, k, :], hT[:, k, TOK - 8:TOK], start=(k == 0), stop=(k == KC - 1)))
                            for k in range(KC)], reads=[f"wr{s_}", "hT1"], writes=[f"ps{b_}"])
                P.op("act", lambda e: e.activation(out=qtail[:, c, 0:3], in_=ps[b_][:, 5:8], func=AF.Copy), reads=[f"ps{b_}"], writes=["qtail"])
                P.op("sp", lambda e: e.dma_start(out=S_conv[li][:, c, :], in_=qtail[:, c, 0:3]), reads=["qtail"], writes=[f"Sconv{li}"], dma=True)
            proj_fm(Wl["w_in"], 512, 4, hrhs, hkeys, 2, lambda c, tt, b: evac_qk(c + 4, tt, b))

        for cg in range(8):
            s = wload(Wl["w_in"], 0, 1024 + cg * 128, 128)
            for half in range(2):
                b = bank()
                fns = []
                for t4 in range(4):
                    tc = half * 4 + t4
                    fns += [(lambda e, k=k, tc=tc, t4=t4: e.matmul(ps[b][:, t4 * 128:(t4 + 1) * 128], hT[:, k, tc * 128:(tc + 1) * 128],
                                                                   wr[s][:, k, :], start=(k == 0), stop=(k == KC - 1)))
                            for k in range(KC)]
                P.op("pe", fns, reads=[f"wr{s}", "hT0", "hT1"], writes=[f"ps{b}"])
                hh, jj = cg // 2, cg % 2
                P.op("act", lambda e, b=b, half=half, hh=hh, jj=jj: e.activation(
                    out=v_m[:, half * 4:(half + 1) * 4, hh, jj * 128:(jj + 1) * 128],
                    in_=ps[b][:].rearrange("p (t c) -> p t c", t=4), func=AF.Copy),
                     reads=[f"ps{b}"], writes=["v_m"])
        s = wload(Wl["w_in"], 0, 3072, 8)
        b = bank()
        fns = []
        for tc in range(8):
            fns += [(lambda e, k=k, tc=tc: e.matmul(ps[b][:, tc * 8:(tc + 1) * 8], hT[:, k, tc * 128:(tc + 1) * 128],
                                                    wr[s][:, k, 0:8], start=(k == 0), stop=(k == KC - 1))) for k in range(KC)]
        P.op("pe", fns, reads=[f"wr{s}", "hT0", "hT1"], writes=[f"ps{b}"])
        P.op("dve", lambda e, b=b: e.tensor_tensor(out=gz[:], in0=ps[b][:, 0:64].rearrange("p (t c) -> p t c", t=8),
                                                   in1=gbias.unsqueeze(1).broadcast_to([128, 8, 8]), op=ALU.add),
             reads=[f"ps{b}", "vecs"], writes=["gz"])
        P.op("act", lambda e: e.activation(out=gpre[:, :, 4:8], in_=gz[:, :, 4:8], func=AF.Exp, scale=-1.0), reads=["gz"], writes=["gpre"])
        P.op("act", lambda e: e.activation(out=lf[:], in_=gpre[:, :, 4:8], func=AF.Ln, bias=1.0), reads=["gpre"], writes=["lf"])
        b = bank()
        P.op("pe", [lambda e: e.matmul(ps[b][:, 0:32], utri, lf[:].rearrange("p t h -> p (t h)"), start=True, stop=True),
                    lambda e: e.matmul(ps[b][:, 32:64], ones_f[:], lf[:].rearrange("p t h -> p (t h)"), start=True, stop=True)],
             reads=["lf", "consts", "ones_f"], writes=[f"ps{b}"])
        bpos = ps[b][:, 0:32].rearrange("p (t h) -> p t h", t=8)
        bend = ps[b][:, 32:64].rearrange("p (t h) -> p t h", t=8)
        P.op("dve", lambda e: e.tensor_scalar(out=nb[:], in0=bpos, scalar1=-1.0, scalar2=None, op0=ALU.mult), reads=[f"ps{b}"], writes=["nb"])
        P.op("dve", lambda e: e.scalar_tensor_tensor(out=app[:], in0=bpos, scalar=lnsct[:, 0:1], in1=gz[:, :, 0:4], op0=ALU.add, op1=ALU.add),
             reads=[f"ps{b}", "gz", "lnsct"], writes=["app"])
        P.op("dve", lambda e: e.tensor_tensor(out=eka[:], in0=gz[:, :, 0:4], in1=bend, op=ALU.subtract), reads=[f"ps{b}", "gz"], writes=["eka"])
        P.op("dve", lambda e: e.tensor_tensor(out=eka[:], in0=eka[:], in1=nb[:], op=ALU.subtract), reads=["eka", "nb"], writes=["eka"])
        P.op("act", lambda e: e.activation(out=eka[:], in_=eka[:], func=AF.Exp), reads=["eka"], writes=["eka"])
        P.op("act", lambda e: e.activation(out=ebend[:], in_=bend, func=AF.Exp, scale=-1.0), reads=[f"ps{b}"], writes=["ebend"])
        def refresh_state():
            P.op("act", lambda e: e.activation(out=C_b[:], in_=C32[:, :, 0:256], func=AF.Copy), reads=["C32"], writes=["C_b"])
            for h in range(4):
                P.op("act", lambda e, h=h: e.activation(out=nrep[:, h, :], in_=ones_f[:], func=AF.Identity, scale=C32[:, h, 256:257]),
                     reads=["C32", "ones_f"], writes=["nrep"])
        if full:
            refresh_state()
        P.barrier()

        banks_ = {}

        def st_A(tc):
            tsl = slice(tc * 128, (tc + 1) * 128)
            if full:
                bB = bank()
                for h in range(4):
                    P.op("dve", lambda e, h=h: e.tensor_scalar(out=Brep[h][:], in0=ones_f[:], scalar1=nb[:, tc, h:h + 1], scalar2=None, op0=ALU.mult),
                         reads=["nb", "ones_f"], writes=[f"Brep{h}"])
                P.op("pe", [(lambda e, h=h: e.matmul(ps[bB][:, h * 128:(h + 1) * 128], Brep[h][:], ident_f, start=True, stop=True)) for h in range(4)],
                     reads=[f"Brep{h}" for h in range(4)] + ["consts"], writes=[f"ps{bB}"])
                P.op("act", lambda e: e.activation(out=Ebt[:].rearrange("p h t -> p (h t)"), in_=ps[bB][:], func=AF.Exp, bias=lnsct[:, 0:1]),
                     reads=[f"ps{bB}", "lnsct"], writes=["Ebt"])
                for h in range(4):
                    P.op("act", lambda e, h=h: e.activation(out=Mm[:, h, :], in_=ps[bB][:, h * 128:(h + 1) * 128], func=AF.Exp,
                                                            bias=app[:, tc, h:h + 1]), reads=[f"ps{bB}", "app"], writes=["Mm"])
                P.op("dve", lambda e: e.tensor_tensor(out=Mm[:], in0=Mm[:], in1=mask01.unsqueeze(1).broadcast_to([128, 4, 128]), op=ALU.mult),
                     reads=["Mm", "consts"], writes=["Mm"])
                bS = bank()
                P.op("pe", [(lambda e, h=h: e.matmul(ps[bS][:, h * 128:(h + 1) * 128], k_m[:, h, tsl], q_m[:, h, tsl], start=True, stop=True))
                            for h in range(4)], reads=["qk_m"], writes=[f"ps{bS}"])
                P.op("dve", lambda e: e.tensor_tensor(out=PT[:].rearrange("p h t -> p (h t)"), in0=ps[bS][:], in1=Mm[:].rearrange("p h t -> p (h t)"), op=ALU.mult),
                     reads=[f"ps{bS}", "Mm"], writes=["PT"])
                P.op("dve", lambda e: e.tensor_tensor(out=qs[:], in0=q_m[:, :, tsl], in1=Ebt[:], op=ALU.mult), reads=["qk_m", "Ebt"], writes=["qs"])
            bT = bank()
            psT = ps[bT][:].bitcast(BF16)
            P.op("pe", [(lambda e, h=h: e.transpose(psT[:, h * 128:(h + 1) * 128], k_m[:, h, tsl], ident_b[:])) for h in range(4)],
                 reads=["qk_m", "ident_b"], writes=[f"ps{bT}"])
            P.op("dve", lambda e: e.tensor_tensor(out=kw[:], in0=psT[:, 0:512].rearrange("p (h d) -> p h d", h=4),
                                                  in1=eka[:, tc, :].unsqueeze(2).broadcast_to([128, 4, 128]), op=ALU.mult),
                 reads=[f"ps{bT}", "eka"], writes=["kw"])

        def st_Bmm(tc):
            if full:
                bN = [bank(), bank()]
                for hp in range(2):
                    fns = []
                    for h2 in range(2):
                        h = hp * 2 + h2
                        for j in range(2):
                            o_ = ps[bN[hp]][:, (h2 * 2 + j) * 128:(h2 * 2 + j + 1) * 128]
                            fns.append(lambda e, o_=o_, h=h, j=j: e.matmul(o_, v_m[:, tc, h, j * 128:(j + 1) * 128], PT[:, h, :], start=True, stop=False))
                            fns.append(lambda e, o_=o_, h=h, j=j: e.matmul(o_, C_b[:, h, j * 128:(j + 1) * 128], qs[:, h, :], start=False, stop=True))
                    P.op("pe", fns, reads=["v_m", "PT", "C_b", "qs"], writes=[f"ps{bN[hp]}"])
                bD = bank()
                fns = []
                for h in range(4):
                    o_ = ps[bD][:, h * 128:(h + 1) * 128]
                    fns.append(lambda e, o_=o_, h=h: e.matmul(o_, ones_b[:], PT[:, h, :], start=True, stop=False))
                    fns.append(lambda e, o_=o_, h=h: e.matmul(o_, nrep[:, h, :], qs[:, h, :], start=False, stop=True))
                P.op("pe", fns, reads=["PT", "nrep", "qs", "ones_b"], writes=[f"ps{bD}"])
                banks_[("N", tc)] = bN
                banks_[("D", tc)] = bD
                reserved.update([bN[0], bN[1], bD])
            bC = [bank(), bank()]
            bn = bank()
            for hp in range(2):
                P.op("pe", [(lambda e, h2=h2: e.matmul(ps[bC[hp]][:, h2 * 256:(h2 + 1) * 256], kw[:, hp * 2 + h2, :], v_m[:, tc, hp * 2 + h2, :],
                                                       start=True, stop=True)) for h2 in range(2)], reads=["kw", "v_m"], writes=[f"ps{bC[hp]}"])
            P.op("pe", [(lambda e, h=h: e.matmul(ps[bn][:, h:h + 1], kw[:, h, :], ones_b[:, 0:1], start=True, stop=True)) for h in range(4)],
                 reads=["kw", "ones_b"], writes=[f"ps{bn}"])
            banks_[("C", tc)] = (bC, bn)

        def st_Supd(tc):
            bC, bn = banks_[("C", tc)]
            for h in range(4):
                P.op("dve", lambda e, h=h: e.scalar_tensor_tensor(out=C32[:, h, 0:256], in0=C32[:, h, 0:256], scalar=ebend[:, tc, h:h + 1],
                                                                  in1=ps[bC[h // 2]][:, (h % 2) * 256:(h % 2 + 1) * 256], op0=ALU.mult, op1=ALU.add),
                     reads=["C32", "ebend", f"ps{bC[h // 2]}"], writes=["C32"])
                P.op("dve", lambda e, h=h: e.scalar_tensor_tensor(out=C32[:, h, 256:257], in0=C32[:, h, 256:257], scalar=ebend[:, tc, h:h + 1],
                                                                  in1=ps[bn][:, h:h + 1], op0=ALU.mult, op1=ALU.add),
                     reads=["C32", "ebend", f"ps{bn}"], writes=["C32"])
            if tc < 7 and full:
                refresh_state()

        def st_Bpost(tc):
            if not full:
                return
            tsl = slice(tc * 128, (tc + 1) * 128)
            bN = banks_[("N", tc)]
            bD = banks_[("D", tc)]
            P.op("act", lambda e: e.activation(out=tmpf[0][:], in_=ps[bD][:], func=AF.Abs), reads=[f"ps{bD}"], writes=["tmpf0"])
            P.op("dve", lambda e: e.tensor_scalar_max(out=tmpf[0][:], in0=tmpf[0][:], scalar1=1.0), reads=["tmpf0"], writes=["tmpf0"])
            P.op("dve", lambda e: e.reciprocal(out=tmpf[1][:], in_=tmpf[0][:]), reads=["tmpf0"], writes=["tmpf1"])
            rden = tmpf[1][:].rearrange("p (h t) -> p h t", h=4)
            for hp in range(2):
                P.op("dve", lambda e, hp=hp: e.tensor_tensor(
                    out=hraw[:, hp * 4:(hp + 1) * 4, :].rearrange("p (h j) t -> p h j t", h=2),
                    in0=ps[bN[hp]][:].rearrange("p (h j t) -> p h j t", h=2, j=2),
                    in1=rden[:, hp * 2:(hp + 1) * 2, :].unsqueeze(2).broadcast_to([128, 2, 2, 128]), op=ALU.mult),
                     reads=[f"ps{bN[hp]}", "tmpf1"], writes=["hraw"])
            P.op("act", lambda e: e.activation(out=sq8[:], in_=hraw[:], func=AF.Square), reads=["hraw"], writes=["sq8"])
            bQ = bank()
            fns = []
            for h in range(4):
                for j in range(2):
                    fns.append(lambda e, h=h, j=j: e.matmul(ps[bQ][:, h * 128:(h + 1) * 128], ones_b[:], sq8[:, h * 2 + j, :], start=(j == 0), stop=(j == 1)))
            P.op("pe", fns, reads=["sq8", "ones_b"], writes=[f"ps{bQ}"])
            P.op("act", lambda e: e.activation(out=rt[:], in_=ps[bQ][:], func=AF.Sqrt, scale=1.0 / 256, bias=epst[:, 0:1]), reads=[f"ps{bQ}"], writes=["rt"])
            P.op("dve", lambda e: e.reciprocal(out=rstd[:], in_=rt[:]), reads=["rt"], writes=["rstd"])
            rs4 = rstd[:].rearrange("p (h t) -> p h t", h=4)
            P.op("dve", lambda e: e.tensor_tensor(out=mo[:, 0:8, tsl].rearrange("p (h j) t -> p h j t", h=4),
                                                  in0=hraw[:].rearrange("p (h j) t -> p h j t", h=4),
                                                  in1=rs4.unsqueeze(2).broadcast_to([128, 4, 2, 128]), op=ALU.mult),
                 reads=["hraw", "rstd"], writes=["mo_m"])
            for b_ in (bN[0], bN[1], bD):
                reserved.discard(b_)

        st_A(0)
        for tc in range(8):
            st_Bmm(tc)
            st_Supd(tc)
            if tc < 7:
                st_A(tc + 1)
            st_Bpost(tc)
        if save:
            P.op("sp", lambda e: e.dma_start(out=S_C[li], in_=C32[:]), reads=["C32"], writes=[f"SC{li}"], dma=True)

        P.barrier()
        P.sb_off = RT
        qa = P.sb("qa", [128, 8, TOK], BF16)
        kT = P.sb("kT", [128, 2, 128 + TOK], BF16)
        v_a = P.sb("v_a", [128, 9, 256], BF16)
        Pt = [[P.sb(f"Pt{a}{bq}", [128, 512], BF16) for bq in range(2)] for a in range(2)]
        ha32 = P.sb("ha32", [128, 512], F32)
        den_sb = P.sb("den_sb", [128, 512], F32)
        sqa = P.sb("sqa", [128, 512], BF16)
        vallo = P.sb("vallo", [128, 64], BF16)
        stk = P.sb("stk", [128, 2, 128], F32)
        stv = P.sb("stv", [128, 256], F32)
        assert P.sb_off <= RT + 40960, P.sb_off - RT
        if init:
            P.op("sp", lambda e: e.dma_start(out=stk[:], in_=S_k[li]), reads=[f"Sk{li}"], writes=["stk"], dma=True)
            P.op("sp", lambda e: e.dma_start(out=stv[:], in_=S_v[li]), reads=[f"Sv{li}"], writes=["stv"], dma=True)
            P.op("dve", lambda e: e.tensor_scalar(out=stv[:], in0=stv[:], scalar1=stval[:, 0:1], scalar2=None, op0=ALU.mult),
                 reads=["stv", "stval"], writes=["stv"])
        else:
            P.op("dve", lambda e: e.memset(stk[:], 0.0), writes=["stk"])
            P.op("dve", lambda e: e.memset(stv[:], 0.0), writes=["stv"])
        P.op("dve", lambda e: e.tensor_copy(out=kT[:, :, 0:128], in_=stk[:]), reads=["stk"], writes=["kT"])
        P.op("dve", lambda e: e.tensor_copy(out=v_a[:, 0, :], in_=stv[:]), reads=["stv"], writes=["v_a"])
        if init:
            P.op("dve", lambda e: e.tensor_scalar(out=vallo[:], in0=ones_f[:, 0:64], scalar1=stval[:, 0:1], scalar2=None, op0=ALU.mult),
                 reads=["stval", "ones_f"], writes=["vallo"])
        else:
            P.op("dve", lambda e: e.memset(vallo[:], 0.0), writes=["vallo"])

        def evac_qa(c, tt, b):
            P.op("act", lambda e: e.activation(out=qa[:, c, tt * 512:(tt + 1) * 512], in_=ps[b][:], func=AF.Copy), reads=[f"ps{b}"], writes=["qa"])
        if full:
            proj_fm(Wl["w_in"], 3080, 8, hrhs, hkeys, 2, evac_qa)

        def evac_ka(c, tt, b):
            P.op("act", lambda e: e.activation(out=kT[:, c, 128 + tt * 512:128 + (tt + 1) * 512], in_=ps[b][:], func=AF.Copy), reads=[f"ps{b}"], writes=["kT"])
            if tt == 1:
                if save:
                    P.op("act", lambda e: e.activation(out=stk[:, c, :], in_=ps[b][:, 384:512], func=AF.Copy), reads=[f"ps{b}"], writes=["stk"])
                    P.op("sp", lambda e: e.dma_start(out=S_k[li][:, c, :], in_=stk[:, c, :]), reads=["stk"], writes=[f"Sk{li}"], dma=True)
        proj_fm(Wl["w_in"], 3080 + 1024, 2, hrhs, hkeys, 2, evac_ka)
        for cg in range(2):
            s = wload(Wl["w_in"], 0, 3080 + 1280 + cg * 128, 128)
            for half in range(2):
                b = bank()
                fns = []
                for t4 in range(4):
                    tc = half * 4 + t4
                    fns += [(lambda e, k=k, tc=tc, t4=t4: e.matmul(ps[b][:, t4 * 128:(t4 + 1) * 128], hT[:, k, tc * 128:(tc + 1) * 128],
                                                                   wr[s][:, k, :], start=(k == 0), stop=(k == KC - 1)))
                            for k in range(KC)]
                P.op("pe", fns, reads=[f"wr{s}", "hT0", "hT1"], writes=[f"ps{b}"])
                P.op("act", lambda e, b=b, half=half, cg=cg: e.activation(
                    out=v_a[:, 1 + half * 4:1 + (half + 1) * 4, cg * 128:(cg + 1) * 128],
                    in_=ps[b][:].rearrange("p (t c) -> p t c", t=4), func=AF.Copy), reads=[f"ps{b}"], writes=["v_a"])
                if half == 1 and save:
                    P.op("act", lambda e, b=b, cg=cg: e.activation(out=stv[:, cg * 128:(cg + 1) * 128], in_=ps[b][:, 384:512], func=AF.Copy),
                         reads=[f"ps{b}"], writes=["stv"])
                    P.op("sp", lambda e, cg=cg: e.dma_start(out=S_v[li][:, cg * 128:(cg + 1) * 128], in_=stv[:, cg * 128:(cg + 1) * 128]),
                         reads=["stv"], writes=[f"Sv{li}"], dma=True)

        if not full:
            P.barrier()
            return
        def ogate_group(c, s_):
            for tt in range(2):
                b = bank()
                P.op("pe", [(lambda e, k=k: e.matmul(ps[b][:], wr[s_][:, k, :], hT[:, k, tt * 512:(tt + 1) * 512], start=(k == 0), stop=(k == KC - 1)))
                            for k in range(KC)], reads=[f"wr{s_}", f"hT{tt}"], writes=[f"ps{b}"])
                q_ = rot("sq", 3)
                P.op("act", lambda e: e.activation(out=sqr[q_][:], in_=ps[b][:], func=AF.Sigmoid), reads=[f"ps{b}"], writes=[f"sq{q_}"])
                P.op("dve", lambda e: e.scalar_tensor_tensor(out=mo[:, c, tt * 512:(tt + 1) * 512], in0=sqr[q_][:], scalar=g_ml[:, c:c + 1],
                                                             in1=mo[:, c, tt * 512:(tt + 1) * 512], op0=ALU.mult, op1=ALU.mult),
                     reads=[f"sq{q_}", "mo_m", "vecs"], writes=["mo_m"])
        og_slot = wload(Wl["w_in"], 0, 2048, 128)
        for n in range(8):
            qsl = slice(n * 128, (n + 1) * 128)
            ogate_group(n, og_slot)
            if n < 7:
                og_slot = wload(Wl["w_in"], 0, 2048 + (n + 1) * 128, 128)
            bSS = bank()
            reserved.add(bSS)
            for kc in range(2):
                for hf in range(2):
                    kvh = kc * 2 + hf
                    psl = slice(hf * 64, (hf + 1) * 64)
                    for kb in range(2):
                        ksl = slice((n + kb) * 128, (n + kb + 1) * 128)
                        b = bank()
                        P.op("pe", lambda e, b=b, ksl=ksl: e.matmul(ps[b][:], kT[psl, kc, ksl], qa[psl, kc * 4:(kc + 1) * 4, qsl], start=True, stop=True),
                             reads=["kT", "qa"], writes=[f"ps{b}"])
                        dist = distp if kb == 0 else distc
                        for g in range(4):
                            head = attn_head_of_slot(kc * 4 + g, hf)
                            slope = 2.0 ** (-(head + 1) / 2.0)
                            i = rot("tmp", 2) if g == 0 else i
                            P.op("dve", lambda e, b=b, g=g, i=i, slope=slope, dist=dist: e.scalar_tensor_tensor(
                                out=tmpf[i][:, g * 128:(g + 1) * 128], in0=dist, scalar=-8.0 * slope, in1=ps[b][:, g * 128:(g + 1) * 128],
                                op0=ALU.mult, op1=ALU.add), reads=[f"ps{b}", "consts"], writes=[f"tmpf{i}"])
                        P.op("act", lambda e, i=i, hf=hf, kb=kb: e.activation(out=Pt[hf][kb][:], in_=tmpf[i][:], func=AF.Exp, scale=0.125),
                             reads=[f"tmpf{i}"], writes=[f"Pt{hf}{kb}"])
                bNn = bank()
                bDd = bank()
                fns = []
                fnd = []
                for hf in range(2):
                    kvh = kc * 2 + hf
                    for kb in range(2):
                        vsl = slice(kvh * 64, (kvh + 1) * 64)
                        fns.append(lambda e, hf=hf, kb=kb, vsl=vsl: e.matmul(ps[bNn][hf * 64:(hf + 1) * 64, :], v_a[:, n + kb, vsl], Pt[hf][kb][:],
                                                                             start=(kb == 0), stop=(kb == 1), tile_position=(0, hf * 64)))
                        lo = vallo[:] if (n == 0 and kb == 0) else ones_b[:, 0:64]
                        fnd.append(lambda e, hf=hf, kb=kb, lo=lo: e.matmul(ps[bDd][hf * 64:(hf + 1) * 64, :], lo, Pt[hf][kb][:],
                                                                           start=(kb == 0), stop=(kb == 1), tile_position=(0, hf * 64)))
                P.op("pe", fns, reads=["v_a", "Pt00", "Pt01", "Pt10", "Pt11"], writes=[f"ps{bNn}"])
                P.op("pe", fnd, reads=["vallo", "ones_b", "Pt00", "Pt01", "Pt10", "Pt11"], writes=[f"ps{bDd}"])
                P.op("dve", lambda e, kc=kc: e.tensor_tensor(out=den_sb[:].rearrange("p (g t) -> p g t", g=4),
                                                             in0=ps[bDd][:].rearrange("p (g t) -> p g t", g=4),
                                                             in1=sinkexp[:, kc * 4:(kc + 1) * 4].unsqueeze(2).broadcast_to([128, 4, 128]), op=ALU.add),
                     reads=[f"ps{bDd}", "sinkexp"], writes=["den_sb"])
                P.op("dve", lambda e: e.reciprocal(out=den_sb[:], in_=den_sb[:]), reads=["den_sb"], writes=["den_sb"])
                P.op("dve", lambda e: e.tensor_tensor(out=ha32[:], in0=ps[bNn][:], in1=den_sb[:], op=ALU.mult), reads=[f"ps{bNn}", "den_sb"], writes=["ha32"])
                P.op("act", lambda e: e.activation(out=sqa[:], in_=ha32[:], func=AF.Square), reads=["ha32"], writes=["sqa"])
                P.op("act", lambda e, kc=kc: e.activation(out=mo[:, 8 + kc * 4:8 + (kc + 1) * 4, qsl], in_=ha32[:].rearrange("p (g t) -> p g t", g=4), func=AF.Copy),
                     reads=["ha32"], writes=["mo_a"])
                P.op("pe", [(lambda e, g=g, kc=kc: e.matmul(ps[bSS][:, 0:128], ones_b[:], sqa[:, g * 128:(g + 1) * 128],
                                                            start=(kc == 0 and g == 0), stop=(kc == 1 and g == 3))) for g in range(4)],
                     reads=["sqa", "ones_b"], writes=[f"ps{bSS}"])
            reserved.clear()
            P.op("act", lambda e: e.activation(out=rt[:, 0:128], in_=ps[bSS][:, 0:128], func=AF.Sqrt, scale=1.0 / 1024, bias=epst[:, 0:1]),
                 reads=[f"ps{bSS}"], writes=["rt"])
            P.op("dve", lambda e: e.reciprocal(out=rstd[:, 0:128], in_=rt[:, 0:128]), reads=["rt"], writes=["rstd"])
            for c in range(8):
                P.op("dve", lambda e, c=c: e.scalar_tensor_tensor(out=mo[:, 8 + c, qsl], in0=mo[:, 8 + c, qsl], scalar=g_at[:, c:c + 1],
                                                                  in1=rstd[:, 0:128], op0=ALU.mult, op1=ALU.mult),
                     reads=["mo_a", "rstd", "vecs"], writes=["mo_a"])

        P.barrier()
        if dbg and li == 0 and not init:
            for c in range(16):
                for tt in range(2):
                    i = rot("tmp", 2)
                    P.op("act", lambda e, c=c, tt=tt, i=i: e.activation(out=tmpf[i][:], in_=mo[:, c, tt * 512:(tt + 1) * 512], func=AF.Copy),
                         reads=["mo_m", "mo_a"], writes=[f"tmpf{i}"])
                    P.op("sp", lambda e, c=c, tt=tt, i=i: e.dma_start(out=dbg_mo_d[:, c, tt * 512:(tt + 1) * 512], in_=tmpf[i][:]),
                         reads=[f"tmpf{i}"], writes=["out"], dma=True)
        pend = []

        def flush_ss(final=False):
            while pend and (final or len(pend) > 1):
                c, tt, s, bss = pend.pop(0)
                P.op("pe", lambda e, c=c, s=s, bss=bss: e.matmul(ps[bss][:], ones_b[:], sqr[s][:], start=(c == 0), stop=(c == KC - 1)),
                     reads=[f"sq{s}"], writes=[f"ps{bss}"])

        bss2 = [bank(), bank()]
        reserved.update(bss2)

        def evac_y(c, tt, b):
            P.op("act", lambda e: e.activation(out=ystash[:, c, tt * 512:(tt + 1) * 512], in_=ps[b][:], func=AF.Copy), reads=[f"ps{b}"], writes=[f"ys{tt}"])
            s = rot("sq", 3)
            P.op("act", lambda e: e.activation(out=sqr[s][:], in_=ps[b][:], func=AF.Square), reads=[f"ps{b}"], writes=[f"sq{s}"])
            pend.append((c, tt, s, bss2[tt]))
            flush_ss()
        proj_fm(Wl["w_out"], 0, KC, lambda k, tt: mo[:, k, tt * 512:(tt + 1) * 512], lambda tt: ["mo_m", "mo_a"], 2, evac_y)
        flush_ss(final=True)
        reserved.clear()

        def residual(tt, bss, src_fn, srckey, ggi, t0):
            P.op("act", lambda e: e.activation(out=rt[:], in_=ps[bss][:], func=AF.Sqrt, scale=1.0 / D, bias=epst[:, 0:1]), reads=[f"ps{bss}"], writes=["rt"])
            P.op("dve", lambda e: e.reciprocal(out=rstd[:], in_=rt[:]), reads=["rt"], writes=["rstd"])
            for c in range(KC):
                i = rot("tmp", 2)
                P.op("dve", lambda e, c=c, i=i: e.tensor_tensor(out=tmpf[i][:], in0=src_fn(c), in1=rstd[:], op=ALU.mult),
                     reads=[srckey, "rstd"], writes=[f"tmpf{i}"])
                P.op("dve", lambda e, c=c, i=i: e.scalar_tensor_tensor(out=xT[:, c, t0:t0 + 512], in0=tmpf[i][:], scalar=der[:, ggi, c:c + 1],
                                                                       in1=xT[:, c, t0:t0 + 512], op0=ALU.mult, op1=ALU.add),
                     reads=[f"tmpf{i}", "der", f"x{c}"], writes=[f"x{c}"])
        for tt in range(2):
            residual(tt, bss2[tt], lambda c, tt=tt: ystash[:, c, tt * 512:(tt + 1) * 512], f"ys{tt}", 1, tt * 512)

        P.barrier()
        if dbg and li == 0 and not init:
            for k in range(KC):
                P.op("sp", lambda e, k=k: e.dma_start(out=dbg_x_d[k * 128:(k + 1) * 128, :], in_=xT[:, k, :]), reads=[f"x{k}"], writes=["out"], dma=True)
        for tt in range(2):
            prenorm(lambda k: hpT[:, k, :], lambda k: "y32", tt * 512, 512, 2, lambda k: shift_m[:, k:k + 1])

            def evac_up(c, _tt, b):
                i = rot("tmp", 2)
                P.op("act", lambda e: e.activation(out=tmpf[i][:], in_=ps[b][:], func=AF.Relu), reads=[f"ps{b}"], writes=[f"tmpf{i}"])
                P.op("dve", lambda e: e.tensor_tensor(out=actT[:, c, :], in0=tmpf[i][:], in1=tmpf[i][:], op=ALU.mult), reads=[f"tmpf{i}"], writes=["actT"])
            proj_fm(Wl["w_up"], 0, 64, lambda k, _tt: hpT[:, k, :], lambda _tt: ["y32"], 1, evac_up)
            bssm = bank()
            reserved.add(bssm)
            pend2 = []
            for c in range(KC):
                b = bank()
                for q in range(4):
                    s = wload(Wl["w_down"], q * KC, c * 128, 128)
                    fns = [(lambda e, k=k, s=s, q=q: e.matmul(ps[b][:], wr[s][:, k, :], actT[:, q * KC + k, :], start=(q == 0 and k == 0),
                                                             stop=(q == 3 and k == KC - 1))) for k in range(KC)]
                    P.op("pe", fns, reads=[f"wr{s}", "actT"], writes=[f"ps{b}"])
                P.op("act", lambda e, c=c, b=b: e.activation(out=y32[:, c, :], in_=ps[b][:], func=AF.Copy), reads=[f"ps{b}"], writes=["y32"])
                s2 = rot("sq", 3)
                P.op("act", lambda e, s2=s2, b=b: e.activation(out=sqr[s2][:], in_=ps[b][:], func=AF.Square), reads=[f"ps{b}"], writes=[f"sq{s2}"])
                pend2.append((c, s2))
                while pend2 and (len(pend2) > 1 or c == KC - 1):
                    c2, s3 = pend2.pop(0)
                    P.op("pe", lambda e, c2=c2, s3=s3: e.matmul(ps[bssm][:], ones_b[:], sqr[s3][:], start=(c2 == 0), stop=(c2 == KC - 1)),
                         reads=[f"sq{s3}"], writes=[f"ps{bssm}"])
            reserved.clear()
            residual(tt, bssm, lambda c: y32[:, c, :], "y32", 3, tt * 512)
        P.barrier()

    emit_unit(0, True, False, True)
    emit_unit(1, False, False, True)
    load_x(xT2_d)
    emit_unit(0, True, True, False)
    emit_unit(1, True, True, False)

    for k in range(KC):
        P.op("sp", lambda e, k=k: e.dma_start(out=outT_d[k * 128:(k + 1) * 128, :], in_=xT[:, k, :]), reads=[f"x{k}"], writes=[f"out{k}"], dma=True)
    P.op("sp", [], reads=["out"] + [f"out{k}" for k in range(KC)])
    P.barrier(engines=("sp",))
    P.emit()
    return nc


def make_consts():
    c = np.zeros((128, NCONST), np.float32)
    c[:, 0:128] = np.eye(128, dtype=np.float32)
    s = np.arange(128)[:, None]
    t = np.arange(128)[None, :]
    c[:, 128:256] = (s <= t).astype(np.float32)
    dp = (t + 128 - s).astype(np.float32)
    c[:, 256:384] = np.where(s > t, dp, 1e6)
    dc = (t - s).astype(np.float32)
    c[:, 384:512] = np.where(s <= t, dc, 1e6)
    c[:, 512:640] = (s <= t).astype(np.float32)
    return c


def col_major(v, n):
    return np.ascontiguousarray(np.asarray(v, np.float32).reshape(n, 128).T)


ATT_PERM = np.concatenate([np.arange(64) + 64 * attn_head_of_slot(c, hf) for c in range(8) for hf in range(2)])


def layer_inputs(i, l, inp):
    w_in = np.array(inp["w_in"][l], np.float32)
    w_in[:, 3080:3080 + 1024] = w_in[:, 3080 + ATT_PERM]
    w_out = np.array(inp["w_out"][l], np.float32)
    w_out[1024:2048, :] = w_out[1024 + ATT_PERM, :]
    g_at = np.asarray(inp["g_attn_out"][l], np.float32)[ATT_PERM]
    sinks = np.asarray(inp["attn_sinks"][l], np.float32)
    vec = np.zeros((128, NVEC), np.float32)
    vec[:, 0:16] = col_major(inp["g_pre_mix"][l], 16)
    vec[:, 16:32] = col_major(inp["g_post_mix"][l], 16)
    vec[:, 32:48] = col_major(inp["g_pre_mlp"][l], 16)
    vec[:, 48:64] = col_major(inp["g_post_mlp"][l], 16)
    cw = np.asarray(inp["conv_w"][l], np.float32)
    for c in range(8):
        for j in range(4):
            vec[:, 64 + c * 4 + j] = cw[j, c * 128:(c + 1) * 128]
    vec[:, 96:104] = col_major(inp["conv_b"][l], 8)
    vec[:, 104:112] = col_major(np.asarray(inp["g_mlstm_head"][l]).reshape(-1), 8)
    vec[:, 112:120] = col_major(g_at, 8)
    for c in range(8):
        vec[0:64, 120 + c] = sinks[attn_head_of_slot(c, 0)]
        vec[64:128, 120 + c] = sinks[attn_head_of_slot(c, 1)]
    vec[:, 128:132] = np.asarray(inp["b_i"][l], np.float32)[None, :]
    vec[:, 132:136] = np.asarray(inp["b_f"][l], np.float32)[None, :]
    return {
        f"w_ada{i}": np.ascontiguousarray(inp["w_ada"][l], dtype=np.float32),
        f"b_ada{i}": col_major(inp["b_ada"][l], 96),
        f"w_in{i}": w_in, f"w_out{i}": w_out,
        f"w_up{i}": np.ascontiguousarray(inp["w_up"][l], dtype=np.float32),
        f"w_down{i}": np.ascontiguousarray(inp["w_down"][l], dtype=np.float32),
        f"vecs{i}": vec,
    }


_NC_CACHE = {}


def get_nc(dbg=False):
    if dbg not in _NC_CACHE:
        _NC_CACHE[dbg] = build_program([0, 1], dbg=dbg)
    return _NC_CACHE[dbg]


def core_inputs(inp, lw, consts, b, h):
    x = inp["x"]
    m = {"xT1": np.ascontiguousarray(np.asarray(x[b, 0:1024, :], np.float32).T),
         "xT2": np.ascontiguousarray(np.asarray(x[b, h * 1024:(h + 1) * 1024, :], np.float32).T),
         "cT": col_major(inp["c"][b], 16), "consts": consts,
         "st_valid": np.full((128, 1), float(h), np.float32)}
    for d in lw:
        m.update(d)
    return m


def kernel(**inp):
    x = np.asarray(inp["x"])
    B = x.shape[0]
    consts = make_consts()
    lw = [layer_inputs(l, l, inp) for l in range(2)]
    nc = get_nc()
    maps = [core_inputs(inp, lw, consts, b, h) for b in range(B) for h in range(2)]
    res = run_bass_kernel_spmd(nc, maps, core_ids=list(range(2 * B))).results
    out = np.empty(x.shape, np.float32)
    for b in range(B):
        for h in range(2):
            out[b, h * 1024:(h + 1) * 1024, :] = res[2 * b + h]["outT"].T
    return out
```

```python
from contextlib import ExitStack
import math
import numpy as np
import concourse.bass as bass
import concourse.mybir as mybir
from concourse.bass_utils import run_bass_kernel_spmd

F32 = mybir.dt.float32
BF16 = mybir.dt.bfloat16
AF = mybir.ActivationFunctionType
ALU = mybir.AluOpType
ENG = ["pe", "act", "dve", "pool", "sp"]
NDMASEM = 24

D = 2048
KC = 16
TOK = 1024
DIN = 4616
DFF = 8192
EPS = 1e-6
NW = 5
NVEC = 136
NCONST = 640
SB_TOP = 229344


class _Rec:
    def __init__(self):
        self.calls = []

    def __getattr__(self, nm):
        def f(*a, **kw):
            self.calls.append((nm, a, kw))
        return f


class Prog:
    def __init__(self, nc):
        self.nc = nc
        self.ops = {e: [] for e in ENG}
        self.cnt = {e: 0 for e in ENG}
        self.seen = {e: {} for e in ENG}
        self.lastw = {}
        self.readers = {}
        self.dma_cnt = [0] * NDMASEM
        self.dma_rr = 0
        self.dma_rr_e = {}
        self.sb_off = 16640
        self.bank_rr = 0
        self.nalloc = 0

    def sb(self, name, shape, dtype, off=None):
        esz = 4 if dtype == F32 else 2
        n = 1
        for s in shape[1:]:
            n *= s
        nbytes = n * esz
        if off is None:
            off = (self.sb_off + 63) // 64 * 64
            self.sb_off = off + nbytes
        assert off + nbytes <= SB_TOP, (name, off, nbytes)
        self.nalloc += 1
        return self.nc.alloc_sbuf_tensor_at(f"{name}_{self.nalloc}", list(shape), dtype, offset=off)

    def op(self, eng, fns, reads=(), writes=(), dma=False):
        waits = {}

        def need(tok):
            if tok is None:
                return
            k, v = tok
            if waits.get(k, 0) < v:
                waits[k] = v

        for key in reads:
            need(self.lastw.get(key))
        for key in writes:
            need(self.lastw.get(key))
            for t in self.readers.get(key, ()):
                need(t)
        wl = []
        for k, v in waits.items():
            if self.seen[eng].get(k, 0) >= v:
                continue
            self.seen[eng][k] = v
            wl.append((k, v))
        if not isinstance(fns, list):
            fns = [fns]
        rec = _Rec()
        for f in fns:
            f(rec)
        fns = rec.calls
        if len(fns) == 0:
            tok = None
        elif dma:
            lo, n = (0, 8) if eng == "pool" else (8, NDMASEM - 8)
            r = self.dma_rr_e.get(eng, 0)
            self.dma_rr_e[eng] = (r + 1) % n
            s = lo + r
            kprev = ("dma", s)
            if self.dma_cnt[s] > 0 and self.seen[eng].get(kprev, 0) < self.dma_cnt[s]:
                self.seen[eng][kprev] = self.dma_cnt[s]
                wl.append((kprev, self.dma_cnt[s]))
            self.dma_cnt[s] += 16
            tok = (("dma", s), self.dma_cnt[s])
        else:
            self.cnt[eng] += 1
            tok = (eng, self.cnt[eng])
        self.ops[eng].append((wl, fns, tok))
        if tok is not None:
            for key in reads:
                self.readers.setdefault(key, []).append(tok)
            for key in writes:
                self.lastw[key] = tok
                self.readers[key] = []
        return tok

    def barrier(self, engines=("pe", "act", "dve", "sp")):
        for e in engines:
            wl = []
            for k in ENG:
                v = self.cnt[k]
                if v > 0 and self.seen[e].get(k, 0) < v:
                    self.seen[e][k] = v
                    wl.append((k, v))
            for s in range(NDMASEM):
                v = self.dma_cnt[s]
                k = ("dma", s)
                if v > 0 and self.seen[e].get(k, 0) < v:
                    self.seen[e][k] = v
                    wl.append((k, v))
            self.ops[e].append((wl, [], None))

    def emit(self):
        nc = self.nc
        with ExitStack() as st:
            sems = {e: st.enter_context(nc.semaphore(f"s_{e}")) for e in ENG}
            dsems = [st.enter_context(nc.semaphore(f"d{i}")) for i in range(NDMASEM)]
            block = st.enter_context(nc.Block())

            def run(name, e):
                for wl, fns, tok in self.ops[name]:
                    for k, v in wl:
                        sem = sems[k] if isinstance(k, str) else dsems[k[1]]
                        e.wait_ge(sem, v)
                    ins = None
                    for (nm, a, kw) in fns:
                        ins = getattr(e, nm)(*a, **kw)
                    if tok is not None:
                        k, v = tok
                        if isinstance(k, str):
                            ins.then_inc(sems[k], 1)
                        else:
                            ins.then_inc(dsems[k[1]], 16)

            @block.tensor
            def _(e):
                run("pe", e)

            @block.scalar
            def _(e):
                run("act", e)

            @block.vector
            def _(e):
                run("dve", e)

            @block.gpsimd
            def _(e):
                run("pool", e)

            @block.sync
            def _(e):
                run("sp", e)


def attn_head_of_slot(c, hf):
    return (c if c < 4 else 8 + (c - 4)) + 4 * hf


def build_program(layers, dbg=False):
    nc = bass.Bass("TRN2", target_bir_lowering=False)
    P = Prog(nc)

    def din(name, shape):
        return nc.dram_tensor(name, list(shape), F32, kind="ExternalInput").ap()

    def dout(name, shape):
        return nc.dram_tensor(name, list(shape), F32, kind="ExternalOutput").ap()

    xT1_d = din("xT1", [D, TOK])
    xT2_d = din("xT2", [D, TOK])
    cT_d = din("cT", [128, KC])
    consts_d = din("consts", [128, NCONST])
    stval_d = din("st_valid", [128, 1])
    W = []
    for i in range(len(layers)):
        W.append(dict(
            w_ada=din(f"w_ada{i}", [D, 6 * D]), b_ada=din(f"b_ada{i}", [128, 96]),
            w_in=din(f"w_in{i}", [D, DIN]), w_out=din(f"w_out{i}", [D, D]),
            w_up=din(f"w_up{i}", [D, DFF]), w_down=din(f"w_down{i}", [DFF, D]),
            vecs=din(f"vecs{i}", [128, NVEC])))
    outT_d = dout("outT", [D, TOK])
    if dbg:
        dbg_mo_d = dout("dbg_mo", [128, 16, TOK])
        dbg_x_d = dout("dbg_x", [D, TOK])
        dbg_h_d = dout("dbg_h", [D, TOK])
        dbg_mod_d = dout("dbg_mod", [128, 96])

    ps = [nc.alloc_psum_tensor(f"ps{i}", [128, 512], F32) for i in range(8)]

    reserved = set()

    def bank():
        while True:
            b = P.bank_rr
            P.bank_rr = (b + 1) % 8
            if b not in reserved:
                return b

    xT = P.sb("xT", [128, KC, TOK], F32)
    consts = P.sb("consts", [128, NCONST], F32)
    ident_f = consts[:, 0:128]
    mask01 = consts[:, 128:256]
    distp = consts[:, 256:384]
    distc = consts[:, 384:512]
    utri = consts[:, 512:640]
    ident_b = P.sb("ident_b", [128, 128], BF16)
    ones_b = P.sb("ones_b", [128, 128], BF16)
    ones_f = P.sb("ones_f", [128, 128], F32)
    vecs = P.sb("vecs", [128, NVEC], F32)
    mod = P.sb("mod", [128, len(layers), 96], F32)
    bada = P.sb("bada", [128, 96], F32)
    der = P.sb("der", [128, 4, KC], F32)
    cact = P.sb("cact", [128, KC], BF16)
    cT = P.sb("cT", [128, KC], F32)
    sinkexp = P.sb("sinkexp", [128, 8], F32)
    epst = P.sb("epst", [128, 1], F32)
    lnsct = P.sb("lnsct", [128, 1], F32)
    stval = P.sb("stval", [128, 1], F32)
    S_C = [nc.dram_tensor(f"S_C{l}", [128, 4, 257], F32).ap() for l in range(2)]
    S_conv = [nc.dram_tensor(f"S_conv{l}", [128, 8, 3], F32).ap() for l in range(2)]
    S_k = [nc.dram_tensor(f"S_k{l}", [128, 2, 128], F32).ap() for l in range(2)]
    S_v = [nc.dram_tensor(f"S_v{l}", [128, 256], F32).ap() for l in range(2)]
    wr = [P.sb(f"wr{s}", [128, KC, 128], BF16) for s in range(NW)]
    sqr = [P.sb(f"sq{s}", [128, 512], BF16) for s in range(3)]
    rt = P.sb("rt", [128, 512], F32)
    rstd = P.sb("rstd", [128, 512], F32)
    tmpf = [P.sb(f"tmpf{s}", [128, 512], F32) for s in range(2)]
    R0 = (P.sb_off + 63) // 64 * 64
    st = {"wr": 0, "sq": 0, "tmp": 0}

    def rot(name, n):
        i = st[name]
        st[name] = (i + 1) % n
        return i

    mo = P.sb("mo", [128, 16, TOK], BF16, off=R0)
    hT = P.sb("hT", [128, KC, TOK], BF16, off=R0 + 32768)
    RT = R0 + 65536
    ystash = P.sb("ystash", [128, KC, TOK], F32, off=R0 + 32768)
    actT = P.sb("actT", [128, 64, 512], BF16, off=R0)
    y32 = P.sb("y32", [128, KC, 512], F32, off=R0 + 65536)
    hpT = P.sb("hpT", [128, KC, 512], BF16, off=R0 + 65536)
    assert R0 + 65536 + 40960 <= SB_TOP, R0
    hpF = P.sb("hpF", [128, KC, TOK], BF16, off=R0)
    yacc = P.sb("yacc", [128, KC, TOK], F32, off=R0 + 32768)
    actb = P.sb("actb", [128, 4, TOK], BF16, off=R0 + 98304)

    def wload(wap, k0, c0, ncols):
        s = rot("wr", NW)
        src = wap[k0 * 128:(k0 + KC) * 128, c0:c0 + ncols].rearrange("(k p) n -> p k n", p=128)
        P.op("pool", lambda e, s=s: e.dma_start(out=wr[s][:, :, 0:ncols], in_=src), writes=[f"wr{s}"], dma=True)
        return s

    def wload_w(wap, k0, nk, c0, ncols):
        s = rot("wr", NW)
        src = wap[k0 * 128:(k0 + nk) * 128, c0:c0 + ncols].rearrange("(k p) n -> p k n", p=128)
        dst = wr[s][:].rearrange("p k n -> p (k n)").rearrange("p (k n) -> p k n", k=nk)
        P.op("pool", lambda e: e.dma_start(out=dst, in_=src), writes=[f"wr{s}"], dma=True)
        return s, dst

    def rms_stats(src_fn, nchunks, width, srckeys, scale):
        b = bank()
        pend = None
        for k in range(nchunks):
            s = rot("sq", 3)
            P.op("act", lambda e, s=s, k=k: e.activation(out=sqr[s][:, 0:width], in_=src_fn(k), func=AF.Square),
                 reads=srckeys(k), writes=[f"sq{s}"])
            if pend is not None:
                pk, ps_ = pend
                P.op("pe", lambda e, ps_=ps_, pk=pk: e.matmul(ps[b][:, 0:width], ones_b[:], sqr[ps_][:, 0:width],
                                                              start=(pk == 0), stop=False),
                     reads=[f"sq{ps_}"], writes=[f"ps{b}"])
            pend = (k, s)
        pk, ps_ = pend
        P.op("pe", lambda e: e.matmul(ps[b][:, 0:width], ones_b[:], sqr[ps_][:, 0:width], start=(pk == 0), stop=True),
             reads=[f"sq{ps_}"], writes=[f"ps{b}"])
        P.op("act", lambda e: e.activation(out=rt[:, 0:width], in_=ps[b][:, 0:width], func=AF.Sqrt, scale=scale, bias=epst[:, 0:1]),
             reads=[f"ps{b}"], writes=["rt"])
        P.op("dve", lambda e: e.reciprocal(out=rstd[:, 0:width], in_=rt[:, 0:width]), reads=["rt"], writes=["rstd"])

    def prenorm(dst, dstkey, t0, width, gsi, shi_ap):
        rms_stats(lambda k: xT[:, k, t0:t0 + width], KC, width, lambda k: [f"x{k}"], 1.0 / D)
        for k in range(KC):
            i = rot("tmp", 2)
            P.op("dve", lambda e, k=k, i=i: e.scalar_tensor_tensor(out=tmpf[i][:, 0:width], in0=xT[:, k, t0:t0 + width],
                                                                   scalar=der[:, gsi, k:k + 1], in1=rstd[:, 0:width],
                                                                   op0=ALU.mult, op1=ALU.mult),
                 reads=[f"x{k}", "rstd", "der"], writes=[f"tmpf{i}"])
            P.op("act", lambda e, k=k, i=i: e.activation(out=dst(k), in_=tmpf[i][:, 0:width], func=AF.Identity,
                                                         bias=shi_ap(k), scale=1.0),
                 reads=[f"tmpf{i}", "mod"], writes=[dstkey(k)])

    def proj_fm(wap, c0, nchunk, rhs_fn, rhskeys, ntt, evac, k0=0, nk=KC):
        for c in range(nchunk):
            s = wload(wap, k0, c0 + c * 128, 128)
            for tt in range(ntt):
                b = bank()
                fns = [(lambda e, k=k: e.matmul(ps[b][:], wr[s][:, k, :], rhs_fn(k, tt), start=(k == 0), stop=(k == nk - 1)))
                       for k in range(nk)]
                P.op("pe", fns, reads=[f"wr{s}"] + rhskeys(tt), writes=[f"ps{b}"])
                evac(c, tt, b)

    P.op("sp", lambda e: e.dma_start(out=consts[:], in_=consts_d), writes=["consts"], dma=True)
    P.op("sp", lambda e: e.dma_start(out=cT[:], in_=cT_d), writes=["cT"], dma=True)
    P.op("sp", lambda e: e.dma_start(out=stval[:], in_=stval_d), writes=["stval"], dma=True)
    def load_x(src):
        for k in range(KC):
            P.op("sp", lambda e, k=k: e.dma_start(out=xT[:, k, :], in_=src[k * 128:(k + 1) * 128, :]), writes=[f"x{k}"], dma=True)
    load_x(xT1_d)
    P.op("dve", lambda e: e.memset(ones_f[:], 1.0), writes=["ones_f"])
    P.op("dve", lambda e: e.memset(ones_b[:], 1.0), writes=["ones_b"])
    P.op("dve", lambda e: e.memset(epst[:], EPS), writes=["epst"])
    P.op("dve", lambda e: e.memset(lnsct[:], math.log(128.0 ** -0.5)), writes=["lnsct"])
    P.op("dve", lambda e: e.tensor_copy(out=ident_b[:], in_=ident_f), reads=["consts"], writes=["ident_b"])
    P.op("act", lambda e: e.activation(out=cact[:], in_=cT[:], func=AF.Silu), reads=["cT"], writes=["cact"])

    for li in range(len(layers)):
        P.op("sp", lambda e, li=li: e.dma_start(out=bada[:], in_=W[li]["b_ada"]), writes=["bada"], dma=True)
        b = bank()
        for j in range(96):
            s = wload(W[li]["w_ada"], 0, j * 128, 128)
            fns = [(lambda e, k=k, s=s, j=j: e.matmul(ps[b][:, j:j + 1], wr[s][:, k, :], cact[:, k:k + 1],
                                                    start=(k == 0), stop=(k == KC - 1))) for k in range(KC)]
            P.op("pe", fns, reads=[f"wr{s}", "cact"], writes=[f"ps{b}"])
        P.op("dve", lambda e, li=li, b=b: e.tensor_tensor(out=mod[:, li, :], in0=ps[b][:, 0:96], in1=bada[:], op=ALU.add),
             reads=[f"ps{b}", "bada"], writes=["mod"])

    def emit_unit(li, full, init, save):
        Wl = W[li]
        P.op("sp", lambda e, li=li: e.dma_start(out=vecs[:], in_=W[li]["vecs"]), writes=["vecs"], dma=True)
        g_pre_mix, g_post_mix, g_pre_mlp, g_post_mlp = (vecs[:, 0:16], vecs[:, 16:32], vecs[:, 32:48], vecs[:, 48:64])
        convw = vecs[:, 64:96]
        convb = vecs[:, 96:104]
        g_ml = vecs[:, 104:112]
        g_at = vecs[:, 112:120]
        sinks2 = vecs[:, 120:128]
        gbias = vecs[:, 128:136]
        shift_a, scale_a, gate_a = mod[:, li, 0:16], mod[:, li, 16:32], mod[:, li, 32:48]
        shift_m, scale_m, gate_m = mod[:, li, 48:64], mod[:, li, 64:80], mod[:, li, 80:96]
        P.op("dve", lambda e: e.scalar_tensor_tensor(out=der[:, 0, :], in0=scale_a, scalar=1.0, in1=g_pre_mix, op0=ALU.add, op1=ALU.mult),
             reads=["mod", "vecs"], writes=["der"])
        P.op("dve", lambda e: e.tensor_tensor(out=der[:, 1, :], in0=gate_a, in1=g_post_mix, op=ALU.mult), reads=["mod", "vecs"], writes=["der"])
        P.op("dve", lambda e: e.scalar_tensor_tensor(out=der[:, 2, :], in0=scale_m, scalar=1.0, in1=g_pre_mlp, op0=ALU.add, op1=ALU.mult),
             reads=["mod", "vecs"], writes=["der"])
        P.op("dve", lambda e: e.tensor_tensor(out=der[:, 3, :], in0=gate_m, in1=g_post_mlp, op=ALU.mult), reads=["mod", "vecs"], writes=["der"])
        P.op("act", lambda e: e.activation(out=sinkexp[:], in_=sinks2, func=AF.Exp), reads=["vecs"], writes=["sinkexp"])

        P.barrier()
        for tt in range(2):
            prenorm(lambda k, tt=tt: hT[:, k, tt * 512:(tt + 1) * 512], lambda k, tt=tt: f"hT{tt}", tt * 512, 512, 0,
                    lambda k: shift_a[:, k:k + 1])
        hkeys = lambda tt: [f"hT{tt}"]
        if dbg and li == 0 and not init:
            P.op("sp", lambda e: e.dma_start(out=dbg_mod_d, in_=mod[:, 0, :]), reads=["mod"], writes=["out"], dma=True)
            for k in range(KC):
                for tt in range(2):
                    i = rot("tmp", 2)
                    P.op("act", lambda e, k=k, tt=tt, i=i: e.activation(out=tmpf[i][:], in_=hT[:, k, tt * 512:(tt + 1) * 512], func=AF.Copy),
                         reads=[f"hT{tt}"], writes=[f"tmpf{i}"])
                    P.op("sp", lambda e, k=k, tt=tt, i=i: e.dma_start(out=dbg_h_d[k * 128:(k + 1) * 128, tt * 512:(tt + 1) * 512], in_=tmpf[i][:]),
                         reads=[f"tmpf{i}"], writes=["out"], dma=True)
        hrhs = lambda k, tt: hT[:, k, tt * 512:(tt + 1) * 512]

        P.sb_off = RT
        q_m = P.sb("q_m", [128, 4, TOK], BF16)
        k_m = P.sb("k_m", [128, 4, TOK], BF16)
        gpre = P.sb("gpre", [128, 8, 8], F32)
        gz = P.sb("gz", [128, 8, 8], F32)
        lf = P.sb("lf", [128, 8, 4], F32)
        nb = P.sb("nb", [128, 8, 4], F32)
        app = P.sb("app", [128, 8, 4], F32)
        eka = P.sb("eka", [128, 8, 4], F32)
        ebend = P.sb("ebend", [128, 8, 4], F32)
        C32 = P.sb("C32", [128, 4, 257], F32)
        C_b = P.sb("C_b", [128, 4, 256], BF16)
        nrep = P.sb("nrep", [128, 4, 128], BF16)
        hraw = P.sb("hraw", [128, 8, 128], F32)
        sq8 = P.sb("sq8", [128, 8, 128], BF16)
        convt = P.sb("convt", [128, 8, 3], F32)
        qtail = P.sb("qtail", [128, 4, 8], F32)
        m_al = P.sb_off
        pre = [P.sb(f"pre{i}", [128, 3 + TOK], F32) for i in range(2)]
        P.sb_off = m_al
        Brep = [P.sb(f"Brep{i}", [128, 128], F32) for i in range(4)]
        Ebt = P.sb("Ebt", [128, 4, 128], F32)
        Mm = P.sb("Mm", [128, 4, 128], F32)
        PT = P.sb("PT", [128, 4, 128], BF16)
        qs = P.sb("qs", [128, 4, 128], BF16)
        kw = P.sb("kw", [128, 4, 128], BF16)
        assert P.sb_off <= RT + 40960 + 2048, P.sb_off - RT
        v_m = P.sb("v_m", [128, 8, 4, 256], BF16, off=R0 + 16384)

        if init:
            P.op("sp", lambda e: e.dma_start(out=convt[:], in_=S_conv[li]), reads=[f"Sconv{li}"], writes=["convt"], dma=True)
            P.op("sp", lambda e: e.dma_start(out=C32[:], in_=S_C[li]), reads=[f"SC{li}"], writes=["C32"], dma=True)
            P.op("dve", lambda e: e.tensor_scalar(out=convt[:], in0=convt[:], scalar1=stval[:, 0:1], scalar2=None, op0=ALU.mult),
                 reads=["convt", "stval"], writes=["convt"])
            P.op("dve", lambda e: e.tensor_scalar(out=C32[:], in0=C32[:], scalar1=stval[:, 0:1], scalar2=None, op0=ALU.mult),
                 reads=["C32", "stval"], writes=["C32"])
        else:
            P.op("dve", lambda e: e.memset(convt[:], 0.0), writes=["convt"])
            P.op("dve", lambda e: e.memset(C32[:], 0.0), writes=["C32"])

        def evac_qk(c, tt, b):
            i = c % 2
            P.op("act", lambda e: e.activation(out=pre[i][:, 3 + tt * 512:3 + (tt + 1) * 512], in_=ps[b][:], func=AF.Copy),
                 reads=[f"ps{b}"], writes=[f"pre{i}"])
            if tt == 1:
                P.op("dve", lambda e: e.tensor_copy(out=pre[i][:, 0:3], in_=convt[:, c, :]), reads=["convt"], writes=[f"pre{i}"])
                P.op("dve", lambda e: e.tensor_scalar(out=cacc[:], in0=pre[i][:, 3:3 + TOK], scalar1=convw[:, c * 4 + 3:c * 4 + 4],
                                                      scalar2=convb[:, c:c + 1], op0=ALU.mult, op1=ALU.add),
                     reads=[f"pre{i}", "vecs"], writes=["cacc"])
                for j in (2, 1, 0):
                    P.op("dve", lambda e, j=j: e.scalar_tensor_tensor(out=cacc[:], in0=pre[i][:, j:j + TOK],
                                                                      scalar=convw[:, c * 4 + j:c * 4 + j + 1], in1=cacc[:],
                                                                      op0=ALU.mult, op1=ALU.add),
                         reads=[f"pre{i}", "vecs", "cacc"], writes=["cacc"])
                dst = q_m[:, c, :] if c < 4 else k_m[:, c - 4, :]
                P.op("act", lambda e: e.activation(out=dst, in_=cacc[:], func=AF.Silu), reads=["cacc"], writes=["qk_m"])
                if save:
                    P.op("sp", lambda e: e.dma_start(out=S_conv[li][:, c, :], in_=pre[i][:, TOK:TOK + 3]), reads=[f"pre{i}"],
                         writes=[f"Sconv{li}"], dma=True)

        cacc = P.sb("cacc", [128, TOK], F32, off=R0)
        if full:
            proj_fm(Wl["w_in"], 0, 8, hrhs, hkeys, 2, evac_qk)
        else:
            for c in range(4):
                s_ = wload(Wl["w_in"], 0, c * 128, 128)
                b_ = bank()
                P.op("pe", [(lambda e, k=k: e.matmul(ps[b_][:, 0:8], wr[s_][:, k, :], hT[:, k, TOK - 8:TOK], start=(k == 0), stop=(k == KC - 1)))
                            for k in range(KC)], reads=[f"wr{s_}", "hT1"], writes=[f"ps{b_}"])
                P.op("act", lambda e: e.activation(out=qtail[:, c, 0:3], in_=ps[b_][:, 5:8], func=AF.Copy), reads=[f"ps{b_}"], writes=["qtail"])
                P.op("sp", lambda e: e.dma_start(out=S_conv[li][:, c, :], in_=qtail[:, c, 0:3]), reads=["qtail"], writes=[f"Sconv{li}"], dma=True)
            proj_fm(Wl["w_in"], 512, 4, hrhs, hkeys, 2, lambda c, tt, b: evac_qk(c + 4, tt, b))

        for cg in range(8):
            s = wload(Wl["w_in"], 0, 1024 + cg * 128, 128)
            for half in range(2):
                b = bank()
                fns = []
                for t4 in range(4):
                    tc = half * 4 + t4
                    fns += [(lambda e, k=k, tc=tc, t4=t4: e.matmul(ps[b][:, t4 * 128:(t4 + 1) * 128], hT[:, k, tc * 128:(tc + 1) * 128],
                                                                   wr[s][:, k, :], start=(k == 0), stop=(k == KC - 1)))
                            for k in range(KC)]
                P.op("pe", fns, reads=[f"wr{s}", "hT0", "hT1"], writes=[f"ps{b}"])
                hh, jj = cg // 2, cg % 2
                P.op("act", lambda e, b=b, half=half, hh=hh, jj=jj: e.activation(
                    out=v_m[:, half * 4:(half + 1) * 4, hh, jj * 128:(jj + 1) * 128],
                    in_=ps[b][:].rearrange("p (t c) -> p t c", t=4), func=AF.Copy),
                     reads=[f"ps{b}"], writes=["v_m"])
        s = wload(Wl["w_in"], 0, 3072, 8)
        b = bank()
        fns = []
        for tc in range(8):
            fns += [(lambda e, k=k, tc=tc: e.matmul(ps[b][:, tc * 8:(tc + 1) * 8], hT[:, k, tc * 128:(tc + 1) * 128],
                                                    wr[s][:, k, 0:8], start=(k == 0), stop=(k == KC - 1))) for k in range(KC)]
        P.op("pe", fns, reads=[f"wr{s}", "hT0", "hT1"], writes=[f"ps{b}"])
        P.op("dve", lambda e, b=b: e.tensor_tensor(out=gz[:], in0=ps[b][:, 0:64].rearrange("p (t c) -> p t c", t=8),
                                                   in1=gbias.unsqueeze(1).broadcast_to([128, 8, 8]), op=ALU.add),
             reads=[f"ps{b}", "vecs"], writes=["gz"])
        P.op("act", lambda e: e.activation(out=gpre[:, :, 4:8], in_=gz[:, :, 4:8], func=AF.Exp, scale=-1.0), reads=["gz"], writes=["gpre"])
        P.op("act", lambda e: e.activation(out=lf[:], in_=gpre[:, :, 4:8], func=AF.Ln, bias=1.0), reads=["gpre"], writes=["lf"])
        b = bank()
        P.op("pe", [lambda e: e.matmul(ps[b][:, 0:32], utri, lf[:].rearrange("p t h -> p (t h)"), start=True, stop=True),
                    lambda e: e.matmul(ps[b][:, 32:64], ones_f[:], lf[:].rearrange("p t h -> p (t h)"), start=True, stop=True)],
             reads=["lf", "consts", "ones_f"], writes=[f"ps{b}"])
        bpos = ps[b][:, 0:32].rearrange("p (t h) -> p t h", t=8)
        bend = ps[b][:, 32:64].rearrange("p (t h) -> p t h", t=8)
        P.op("dve", lambda e: e.tensor_scalar(out=nb[:], in0=bpos, scalar1=-1.0, scalar2=None, op0=ALU.mult), reads=[f"ps{b}"], writes=["nb"])
        P.op("dve", lambda e: e.scalar_tensor_tensor(out=app[:], in0=bpos, scalar=lnsct[:, 0:1], in1=gz[:, :, 0:4], op0=ALU.add, op1=ALU.add),
             reads=[f"ps{b}", "gz", "lnsct"], writes=["app"])
        P.op("dve", lambda e: e.tensor_tensor(out=eka[:], in0=gz[:, :, 0:4], in1=bend, op=ALU.subtract), reads=[f"ps{b}", "gz"], writes=["eka"])
        P.op("dve", lambda e: e.tensor_tensor(out=eka[:], in0=eka[:], in1=nb[:], op=ALU.subtract), reads=["eka", "nb"], writes=["eka"])
        P.op("act", lambda e: e.activation(out=eka[:], in_=eka[:], func=AF.Exp), reads=["eka"], writes=["eka"])
        P.op("act", lambda e: e.activation(out=ebend[:], in_=bend, func=AF.Exp, scale=-1.0), reads=[f"ps{b}"], writes=["ebend"])
        def refresh_state():
            P.op("act", lambda e: e.activation(out=C_b[:], in_=C32[:, :, 0:256], func=AF.Copy), reads=["C32"], writes=["C_b"])
            for h in range(4):
                P.op("act", lambda e, h=h: e.activation(out=nrep[:, h, :], in_=ones_f[:], func=AF.Identity, scale=C32[:, h, 256:257]),
                     reads=["C32", "ones_f"], writes=["nrep"])
        if full:
            refresh_state()
        P.barrier()

        banks_ = {}

        def st_A(tc):
            tsl = slice(tc * 128, (tc + 1) * 128)
            if full:
                bB = bank()
                for h in range(4):
                    P.op("dve", lambda e, h=h: e.tensor_scalar(out=Brep[h][:], in0=ones_f[:], scalar1=nb[:, tc, h:h + 1], scalar2=None, op0=ALU.mult),
                         reads=["nb", "ones_f"], writes=[f"Brep{h}"])
                P.op("pe", [(lambda e, h=h: e.matmul(ps[bB][:, h * 128:(h + 1) * 128], Brep[h][:], ident_f, start=True, stop=True)) for h in range(4)],
                     reads=[f"Brep{h}" for h in range(4)] + ["consts"], writes=[f"ps{bB}"])
                P.op("act", lambda e: e.activation(out=Ebt[:].rearrange("p h t -> p (h t)"), in_=ps[bB][:], func=AF.Exp, bias=lnsct[:, 0:1]),
                     reads=[f"ps{bB}", "lnsct"], writes=["Ebt"])
                for h in range(4):
                    P.op("act", lambda e, h=h: e.activation(out=Mm[:, h, :], in_=ps[bB][:, h * 128:(h + 1) * 128], func=AF.Exp,
                                                            bias=app[:, tc, h:h + 1]), reads=[f"ps{bB}", "app"], writes=["Mm"])
                P.op("dve", lambda e: e.tensor_tensor(out=Mm[:], in0=Mm[:], in1=mask01.unsqueeze(1).broadcast_to([128, 4, 128]), op=ALU.mult),
                     reads=["Mm", "consts"], writes=["Mm"])
                bS = bank()
                P.op("pe", [(lambda e, h=h: e.matmul(ps[bS][:, h * 128:(h + 1) * 128], k_m[:, h, tsl], q_m[:, h, tsl], start=True, stop=True))
                            for h in range(4)], reads=["qk_m"], writes=[f"ps{bS}"])
                P.op("dve", lambda e: e.tensor_tensor(out=PT[:].rearrange("p h t -> p (h t)"), in0=ps[bS][:], in1=Mm[:].rearrange("p h t -> p (h t)"), op=ALU.mult),
                     reads=[f"ps{bS}", "Mm"], writes=["PT"])
                P.op("dve", lambda e: e.tensor_tensor(out=qs[:], in0=q_m[:, :, tsl], in1=Ebt[:], op=ALU.mult), reads=["qk_m", "Ebt"], writes=["qs"])
            bT = bank()
            psT = ps[bT][:].bitcast(BF16)
            P.op("pe", [(lambda e, h=h: e.transpose(psT[:, h * 128:(h + 1) * 128], k_m[:, h, tsl], ident_b[:])) for h in range(4)],
                 reads=["qk_m", "ident_b"], writes=[f"ps{bT}"])
            P.op("dve", lambda e: e.tensor_tensor(out=kw[:], in0=psT[:, 0:512].rearrange("p (h d) -> p h d", h=4),
                                                  in1=eka[:, tc, :].unsqueeze(2).broadcast_to([128, 4, 128]), op=ALU.mult),
                 reads=[f"ps{bT}", "eka"], writes=["kw"])

        def st_Bmm(tc):
            if full:
                bN = [bank(), bank()]
                for hp in range(2):
                    fns = []
                    for h2 in range(2):
                        h = hp * 2 + h2
                        for j in range(2):
                            o_ = ps[bN[hp]][:, (h2 * 2 + j) * 128:(h2 * 2 + j + 1) * 128]
                            fns.append(lambda e, o_=o_, h=h, j=j: e.matmul(o_, v_m[:, tc, h, j * 128:(j + 1) * 128], PT[:, h, :], start=True, stop=False))
                            fns.append(lambda e, o_=o_, h=h, j=j: e.matmul(o_, C_b[:, h, j * 128:(j + 1) * 128], qs[:, h, :], start=False, stop=True))
                    P.op("pe", fns, reads=["v_m", "PT", "C_b", "qs"], writes=[f"ps{bN[hp]}"])
                bD = bank()
                fns = []
                for h in range(4):
                    o_ = ps[bD][:, h * 128:(h + 1) * 128]
                    fns.append(lambda e, o_=o_, h=h: e.matmul(o_, ones_b[:], PT[:, h, :], start=True, stop=False))
                    fns.append(lambda e, o_=o_, h=h: e.matmul(o_, nrep[:, h, :], qs[:, h, :], start=False, stop=True))
                P.op("pe", fns, reads=["PT", "nrep", "qs", "ones_b"], writes=[f"ps{bD}"])
                banks_[("N", tc)] = bN
                banks_[("D", tc)] = bD
                reserved.update([bN[0], bN[1], bD])
            bC = [bank(), bank()]
            bn = bank()
            for hp in range(2):
                P.op("pe", [(lambda e, h2=h2: e.matmul(ps[bC[hp]][:, h2 * 256:(h2 + 1) * 256], kw[:, hp * 2 + h2, :], v_m[:, tc, hp * 2 + h2, :],
                                                       start=True, stop=True)) for h2 in range(2)], reads=["kw", "v_m"], writes=[f"ps{bC[hp]}"])
            P.op("pe", [(lambda e, h=h: e.matmul(ps[bn][:, h:h + 1], kw[:, h, :], ones_b[:, 0:1], start=True, stop=True)) for h in range(4)],
                 reads=["kw", "ones_b"], writes=[f"ps{bn}"])
            banks_[("C", tc)] = (bC, bn)

        def st_Supd(tc):
            bC, bn = banks_[("C", tc)]
            for h in range(4):
                P.op("dve", lambda e, h=h: e.scalar_tensor_tensor(out=C32[:, h, 0:256], in0=C32[:, h, 0:256], scalar=ebend[:, tc, h:h + 1],
                                                                  in1=ps[bC[h // 2]][:, (h % 2) * 256:(h % 2 + 1) * 256], op0=ALU.mult, op1=ALU.add),
                     reads=["C32", "ebend", f"ps{bC[h // 2]}"], writes=["C32"])
                P.op("dve", lambda e, h=h: e.scalar_tensor_tensor(out=C32[:, h, 256:257], in0=C32[:, h, 256:257], scalar=ebend[:, tc, h:h + 1],
                                                                  in1=ps[bn][:, h:h + 1], op0=ALU.mult, op1=ALU.add),
                     reads=["C32", "ebend", f"ps{bn}"], writes=["C32"])
            if tc < 7 and full:
                refresh_state()

        def st_Bpost(tc):
            if not full:
                return
            tsl = slice(tc * 128, (tc + 1) * 128)
            bN = banks_[("N", tc)]
            bD = banks_[("D", tc)]
            P.op("act", lambda e: e.activation(out=tmpf[0][:], in_=ps[bD][:], func=AF.Abs), reads=[f"ps{bD}"], writes=["tmpf0"])
            P.op("dve", lambda e: e.tensor_scalar_max(out=tmpf[0][:], in0=tmpf[0][:], scalar1=1.0), reads=["tmpf0"], writes=["tmpf0"])
            P.op("dve", lambda e: e.reciprocal(out=tmpf[1][:], in_=tmpf[0][:]), reads=["tmpf0"], writes=["tmpf1"])
            rden = tmpf[1][:].rearrange("p (h t) -> p h t", h=4)
            for hp in range(2):
                P.op("dve", lambda e, hp=hp: e.tensor_tensor(
                    out=hraw[:, hp * 4:(hp + 1) * 4, :].rearrange("p (h j) t -> p h j t", h=2),
                    in0=ps[bN[hp]][:].rearrange("p (h j t) -> p h j t", h=2, j=2),
                    in1=rden[:, hp * 2:(hp + 1) * 2, :].unsqueeze(2).broadcast_to([128, 2, 2, 128]), op=ALU.mult),
                     reads=[f"ps{bN[hp]}", "tmpf1"], writes=["hraw"])
            P.op("act", lambda e: e.activation(out=sq8[:], in_=hraw[:], func=AF.Square), reads=["hraw"], writes=["sq8"])
            bQ = bank()
            fns = []
            for h in range(4):
                for j in range(2):
                    fns.append(lambda e, h=h, j=j: e.matmul(ps[bQ][:, h * 128:(h + 1) * 128], ones_b[:], sq8[:, h * 2 + j, :], start=(j == 0), stop=(j == 1)))
            P.op("pe", fns, reads=["sq8", "ones_b"], writes=[f"ps{bQ}"])
            P.op("act", lambda e: e.activation(out=rt[:], in_=ps[bQ][:], func=AF.Sqrt, scale=1.0 / 256, bias=epst[:, 0:1]), reads=[f"ps{bQ}"], writes=["rt"])
            P.op("dve", lambda e: e.reciprocal(out=rstd[:], in_=rt[:]), reads=["rt"], writes=["rstd"])
            rs4 = rstd[:].rearrange("p (h t) -> p h t", h=4)
            P.op("dve", lambda e: e.tensor_tensor(out=mo[:, 0:8, tsl].rearrange("p (h j) t -> p h j t", h=4),
                                                  in0=hraw[:].rearrange("p (h j) t -> p h j t", h=4),
                                                  in1=rs4.unsqueeze(2).broadcast_to([128, 4, 2, 128]), op=ALU.mult),
                 reads=["hraw", "rstd"], writes=["mo_m"])
            for b_ in (bN[0], bN[1], bD):
                reserved.discard(b_)

        st_A(0)
        for tc in range(8):
            st_Bmm(tc)
            st_Supd(tc)
            if tc < 7:
                st_A(tc + 1)
            st_Bpost(tc)
        if save:
            P.op("sp", lambda e: e.dma_start(out=S_C[li], in_=C32[:]), reads=["C32"], writes=[f"SC{li}"], dma=True)

        P.barrier()
        P.sb_off = RT
        qa = P.sb("qa", [128, 8, TOK], BF16)
        kT = P.sb("kT", [128, 2, 128 + TOK], BF16)
        v_a = P.sb("v_a", [128, 9, 256], BF16)
        Pt = [[P.sb(f"Pt{a}{bq}", [128, 512], BF16) for bq in range(2)] for a in range(2)]
        ha32 = P.sb("ha32", [128, 512], F32)
        den_sb = P.sb("den_sb", [128, 512], F32)
        sqa = P.sb("sqa", [128, 512], BF16)
        vallo = P.sb("vallo", [128, 64], BF16)
        stk = P.sb("stk", [128, 2, 128], F32)
        stv = P.sb("stv", [128, 256], F32)
        assert P.sb_off <= RT + 40960, P.sb_off - RT
        if init:
            P.op("sp", lambda e: e.dma_start(out=stk[:], in_=S_k[li]), reads=[f"Sk{li}"], writes=["stk"], dma=True)
            P.op("sp", lambda e: e.dma_start(out=stv[:], in_=S_v[li]), reads=[f"Sv{li}"], writes=["stv"], dma=True)
            P.op("dve", lambda e: e.tensor_scalar(out=stv[:], in0=stv[:], scalar1=stval[:, 0:1], scalar2=None, op0=ALU.mult),
                 reads=["stv", "stval"], writes=["stv"])
        else:
            P.op("dve", lambda e: e.memset(stk[:], 0.0), writes=["stk"])
            P.op("dve", lambda e: e.memset(stv[:], 0.0), writes=["stv"])
        P.op("dve", lambda e: e.tensor_copy(out=kT[:, :, 0:128], in_=stk[:]), reads=["stk"], writes=["kT"])
        P.op("dve", lambda e: e.tensor_copy(out=v_a[:, 0, :], in_=stv[:]), reads=["stv"], writes=["v_a"])
        if init:
            P.op("dve", lambda e: e.tensor_scalar(out=vallo[:], in0=ones_f[:, 0:64], scalar1=stval[:, 0:1], scalar2=None, op0=ALU.mult),
                 reads=["stval", "ones_f"], writes=["vallo"])
        else:
            P.op("dve", lambda e: e.memset(vallo[:], 0.0), writes=["vallo"])

        def evac_qa(c, tt, b):
            P.op("act", lambda e: e.activation(out=qa[:, c, tt * 512:(tt + 1) * 512], in_=ps[b][:], func=AF.Copy), reads=[f"ps{b}"], writes=["qa"])
        if full:
            proj_fm(Wl["w_in"], 3080, 8, hrhs, hkeys, 2, evac_qa)

        def evac_ka(c, tt, b):
            P.op("act", lambda e: e.activation(out=kT[:, c, 128 + tt * 512:128 + (tt + 1) * 512], in_=ps[b][:], func=AF.Copy), reads=[f"ps{b}"], writes=["kT"])
            if tt == 1:
                if save:
                    P.op("act", lambda e: e.activation(out=stk[:, c, :], in_=ps[b][:, 384:512], func=AF.Copy), reads=[f"ps{b}"], writes=["stk"])
                    P.op("sp", lambda e: e.dma_start(out=S_k[li][:, c, :], in_=stk[:, c, :]), reads=["stk"], writes=[f"Sk{li}"], dma=True)
        proj_fm(Wl["w_in"], 3080 + 1024, 2, hrhs, hkeys, 2, evac_ka)
        for cg in range(2):
            s = wload(Wl["w_in"], 0, 3080 + 1280 + cg * 128, 128)
            for half in range(2):
                b = bank()
                fns = []
                for t4 in range(4):
                    tc = half * 4 + t4
                    fns += [(lambda e, k=k, tc=tc, t4=t4: e.matmul(ps[b][:, t4 * 128:(t4 + 1) * 128], hT[:, k, tc * 128:(tc + 1) * 128],
                                                                   wr[s][:, k, :], start=(k == 0), stop=(k == KC - 1)))
                            for k in range(KC)]
                P.op("pe", fns, reads=[f"wr{s}", "hT0", "hT1"], writes=[f"ps{b}"])
                P.op("act", lambda e, b=b, half=half, cg=cg: e.activation(
                    out=v_a[:, 1 + half * 4:1 + (half + 1) * 4, cg * 128:(cg + 1) * 128],
                    in_=ps[b][:].rearrange("p (t c) -> p t c", t=4), func=AF.Copy), reads=[f"ps{b}"], writes=["v_a"])
                if half == 1 and save:
                    P.op("act", lambda e, b=b, cg=cg: e.activation(out=stv[:, cg * 128:(cg + 1) * 128], in_=ps[b][:, 384:512], func=AF.Copy),
                         reads=[f"ps{b}"], writes=["stv"])
                    P.op("sp", lambda e, cg=cg: e.dma_start(out=S_v[li][:, cg * 128:(cg + 1) * 128], in_=stv[:, cg * 128:(cg + 1) * 128]),
                         reads=["stv"], writes=[f"Sv{li}"], dma=True)

        if not full:
            P.barrier()
            return
        def ogate_group(c, s_):
            for tt in range(2):
                b = bank()
                P.op("pe", [(lambda e, k=k: e.matmul(ps[b][:], wr[s_][:, k, :], hT[:, k, tt * 512:(tt + 1) * 512], start=(k == 0), stop=(k == KC - 1)))
                            for k in range(KC)], reads=[f"wr{s_}", f"hT{tt}"], writes=[f"ps{b}"])
                q_ = rot("sq", 3)
                P.op("act", lambda e: e.activation(out=sqr[q_][:], in_=ps[b][:], func=AF.Sigmoid), reads=[f"ps{b}"], writes=[f"sq{q_}"])
                P.op("dve", lambda e: e.scalar_tensor_tensor(out=mo[:, c, tt * 512:(tt + 1) * 512], in0=sqr[q_][:], scalar=g_ml[:, c:c + 1],
                                                             in1=mo[:, c, tt * 512:(tt + 1) * 512], op0=ALU.mult, op1=ALU.mult),
                     reads=[f"sq{q_}", "mo_m", "vecs"], writes=["mo_m"])
        og_slot = wload(Wl["w_in"], 0, 2048, 128)
        for n in range(8):
            qsl = slice(n * 128, (n + 1) * 128)
            ogate_group(n, og_slot)
            if n < 7:
                og_slot = wload(Wl["w_in"], 0, 2048 + (n + 1) * 128, 128)
            bSS = bank()
            reserved.add(bSS)
            for kc in range(2):
                for hf in range(2):
                    kvh = kc * 2 + hf
                    psl = slice(hf * 64, (hf + 1) * 64)
                    for kb in range(2):
                        ksl = slice((n + kb) * 128, (n + kb + 1) * 128)
                        b = bank()
                        P.op("pe", lambda e, b=b, ksl=ksl: e.matmul(ps[b][:], kT[psl, kc, ksl], qa[psl, kc * 4:(kc + 1) * 4, qsl], start=True, stop=True),
                             reads=["kT", "qa"], writes=[f"ps{b}"])
                        dist = distp if kb == 0 else distc
                        for g in range(4):
                            head = attn_head_of_slot(kc * 4 + g, hf)
                            slope = 2.0 ** (-(head + 1) / 2.0)
                            i = rot("tmp", 2) if g == 0 else i
                            P.op("dve", lambda e, b=b, g=g, i=i, slope=slope, dist=dist: e.scalar_tensor_tensor(
                                out=tmpf[i][:, g * 128:(g + 1) * 128], in0=dist, scalar=-8.0 * slope, in1=ps[b][:, g * 128:(g + 1) * 128],
                                op0=ALU.mult, op1=ALU.add), reads=[f"ps{b}", "consts"], writes=[f"tmpf{i}"])
                        P.op("act", lambda e, i=i, hf=hf, kb=kb: e.activation(out=Pt[hf][kb][:], in_=tmpf[i][:], func=AF.Exp, scale=0.125),
                             reads=[f"tmpf{i}"], writes=[f"Pt{hf}{kb}"])
                bNn = bank()
                bDd = bank()
                fns = []
                fnd = []
                for hf in range(2):
                    kvh = kc * 2 + hf
                    for kb in range(2):
                        vsl = slice(kvh * 64, (kvh + 1) * 64)
                        fns.append(lambda e, hf=hf, kb=kb, vsl=vsl: e.matmul(ps[bNn][hf * 64:(hf + 1) * 64, :], v_a[:, n + kb, vsl], Pt[hf][kb][:],
                                                                             start=(kb == 0), stop=(kb == 1), tile_position=(0, hf * 64)))
                        lo = vallo[:] if (n == 0 and kb == 0) else ones_b[:, 0:64]
                        fnd.append(lambda e, hf=hf, kb=kb, lo=lo: e.matmul(ps[bDd][hf * 64:(hf + 1) * 64, :], lo, Pt[hf][kb][:],
                                                                           start=(kb == 0), stop=(kb == 1), tile_position=(0, hf * 64)))
                P.op("pe", fns, reads=["v_a", "Pt00", "Pt01", "Pt10", "Pt11"], writes=[f"ps{bNn}"])
                P.op("pe", fnd, reads=["vallo", "ones_b", "Pt00", "Pt01", "Pt10", "Pt11"], writes=[f"ps{bDd}"])
                P.op("dve", lambda e, kc=kc: e.tensor_tensor(out=den_sb[:].rearrange("p (g t) -> p g t", g=4),
                                                             in0=ps[bDd][:].rearrange("p (g t) -> p g t", g=4),
                                                             in1=sinkexp[:, kc * 4:(kc + 1) * 4].unsqueeze(2).broadcast_to([128, 4, 128]), op=ALU.add),
                     reads=[f"ps{bDd}", "sinkexp"], writes=["den_sb"])
                P.op("dve", lambda e: e.reciprocal(out=den_sb[:], in_=den_sb[:]), reads=["den_sb"], writes=["den_sb"])
                P.op("dve", lambda e: e.tensor_tensor(out=ha32[:], in0=ps[bNn][:], in1=den_sb[:], op=ALU.mult), reads=[f"ps{bNn}", "den_sb"], writes=["ha32"])
                P.op("act", lambda e: e.activation(out=sqa[:], in_=ha32[:], func=AF.Square), reads=["ha32"], writes=["sqa"])
                P.op("act", lambda e, kc=kc: e.activation(out=mo[:, 8 + kc * 4:8 + (kc + 1) * 4, qsl], in_=ha32[:].rearrange("p (g t) -> p g t", g=4), func=AF.Copy),
                     reads=["ha32"], writes=["mo_a"])
                P.op("pe", [(lambda e, g=g, kc=kc: e.matmul(ps[bSS][:, 0:128], ones_b[:], sqa[:, g * 128:(g + 1) * 128],
                                                            start=(kc == 0 and g == 0), stop=(kc == 1 and g == 3))) for g in range(4)],
                     reads=["sqa", "ones_b"], writes=[f"ps{bSS}"])
            reserved.clear()
            P.op("act", lambda e: e.activation(out=rt[:, 0:128], in_=ps[bSS][:, 0:128], func=AF.Sqrt, scale=1.0 / 1024, bias=epst[:, 0:1]),
                 reads=[f"ps{bSS}"], writes=["rt"])
            P.op("dve", lambda e: e.reciprocal(out=rstd[:, 0:128], in_=rt[:, 0:128]), reads=["rt"], writes=["rstd"])
            for c in range(8):
                P.op("dve", lambda e, c=c: e.scalar_tensor_tensor(out=mo[:, 8 + c, qsl], in0=mo[:, 8 + c, qsl], scalar=g_at[:, c:c + 1],
                                                                  in1=rstd[:, 0:128], op0=ALU.mult, op1=ALU.mult),
                     reads=["mo_a", "rstd", "vecs"], writes=["mo_a"])

        P.barrier()
        if dbg and li == 0 and not init:
            for c in range(16):
                for tt in range(2):
                    i = rot("tmp", 2)
                    P.op("act", lambda e, c=c, tt=tt, i=i: e.activation(out=tmpf[i][:], in_=mo[:, c, tt * 512:(tt + 1) * 512], func=AF.Copy),
                         reads=["mo_m", "mo_a"], writes=[f"tmpf{i}"])
                    P.op("sp", lambda e, c=c, tt=tt, i=i: e.dma_start(out=dbg_mo_d[:, c, tt * 512:(tt + 1) * 512], in_=tmpf[i][:]),
                         reads=[f"tmpf{i}"], writes=["out"], dma=True)
        pend = []

        def flush_ss(final=False):
            while pend and (final or len(pend) > 1):
                c, tt, s, bss = pend.pop(0)
                P.op("pe", lambda e, c=c, s=s, bss=bss: e.matmul(ps[bss][:], ones_b[:], sqr[s][:], start=(c == 0), stop=(c == KC - 1)),
                     reads=[f"sq{s}"], writes=[f"ps{bss}"])

        bss2 = [bank(), bank()]
        reserved.update(bss2)

        def evac_y(c, tt, b):
            P.op("act", lambda e: e.activation(out=ystash[:, c, tt * 512:(tt + 1) * 512], in_=ps[b][:], func=AF.Copy), reads=[f"ps{b}"], writes=[f"ys{tt}"])
            s = rot("sq", 3)
            P.op("act", lambda e: e.activation(out=sqr[s][:], in_=ps[b][:], func=AF.Square), reads=[f"ps{b}"], writes=[f"sq{s}"])
            pend.append((c, tt, s, bss2[tt]))
            flush_ss()
        proj_fm(Wl["w_out"], 0, KC, lambda k, tt: mo[:, k, tt * 512:(tt + 1) * 512], lambda tt: ["mo_m", "mo_a"], 2, evac_y)
        flush_ss(final=True)
        reserved.clear()

        def residual(tt, bss, src_fn, srckey, ggi, t0):
            P.op("act", lambda e: e.activation(out=rt[:], in_=ps[bss][:], func=AF.Sqrt, scale=1.0 / D, bias=epst[:, 0:1]), reads=[f"ps{bss}"], writes=["rt"])
            P.op("dve", lambda e: e.reciprocal(out=rstd[:], in_=rt[:]), reads=["rt"], writes=["rstd"])
            for c in range(KC):
                i = rot("tmp", 2)
                P.op("dve", lambda e, c=c, i=i: e.tensor_tensor(out=tmpf[i][:], in0=src_fn(c), in1=rstd[:], op=ALU.mult),
                     reads=[srckey, "rstd"], writes=[f"tmpf{i}"])
                P.op("dve", lambda e, c=c, i=i: e.scalar_tensor_tensor(out=xT[:, c, t0:t0 + 512], in0=tmpf[i][:], scalar=der[:, ggi, c:c + 1],
                                                                       in1=xT[:, c, t0:t0 + 512], op0=ALU.mult, op1=ALU.add),
                     reads=[f"tmpf{i}", "der", f"x{c}"], writes=[f"x{c}"])
        for tt in range(2):
            residual(tt, bss2[tt], lambda c, tt=tt: ystash[:, c, tt * 512:(tt + 1) * 512], f"ys{tt}", 1, tt * 512)

        P.barrier()
        if dbg and li == 0 and not init:
            for k in range(KC):
                P.op("sp", lambda e, k=k: e.dma_start(out=dbg_x_d[k * 128:(k + 1) * 128, :], in_=xT[:, k, :]), reads=[f"x{k}"], writes=["out"], dma=True)
        for tt in range(2):
            prenorm(lambda k, tt=tt: hpF[:, k, tt * 512:(tt + 1) * 512], lambda k, tt=tt: f"hpF{tt}", tt * 512, 512, 2,
                    lambda k: shift_m[:, k:k + 1])
        NFB = 16
        for fb in range(NFB):
            for c4 in range(4):
                s_ = wload(Wl["w_up"], 0, (fb * 4 + c4) * 128, 128)
                for tt in range(2):
                    b = bank()
                    P.op("pe", [(lambda e, k=k: e.matmul(ps[b][:], wr[s_][:, k, :], hpF[:, k, tt * 512:(tt + 1) * 512], start=(k == 0), stop=(k == KC - 1)))
                                for k in range(KC)], reads=[f"wr{s_}", f"hpF{tt}"], writes=[f"ps{b}"])
                    i = rot("tmp", 2)
                    P.op("act", lambda e: e.activation(out=tmpf[i][:], in_=ps[b][:], func=AF.Relu), reads=[f"ps{b}"], writes=[f"tmpf{i}"])
                    P.op("dve", lambda e: e.tensor_tensor(out=actb[:, c4, tt * 512:(tt + 1) * 512], in0=tmpf[i][:], in1=tmpf[i][:], op=ALU.mult),
                         reads=[f"tmpf{i}"], writes=["actb"])
            for quarter in range(4):
                s_, wv = wload_w(Wl["w_down"], fb * 4, 4, quarter * 512, 512)
                for j4 in range(4):
                    j = quarter * 4 + j4
                    for tt in range(2):
                        b = bank()
                        P.op("pe", [(lambda e, c4=c4: e.matmul(ps[b][:], wv[:, c4, j4 * 128:(j4 + 1) * 128], actb[:, c4, tt * 512:(tt + 1) * 512],
                                                               start=(c4 == 0), stop=(c4 == 3))) for c4 in range(4)],
                             reads=[f"wr{s_}", "actb"], writes=[f"ps{b}"])
                        if fb == 0:
                            P.op("act", lambda e: e.activation(out=yacc[:, j, tt * 512:(tt + 1) * 512], in_=ps[b][:], func=AF.Copy),
                                 reads=[f"ps{b}"], writes=[f"yacc{j}_{tt}"])
                        else:
                            P.op("dve", lambda e: e.tensor_tensor(out=yacc[:, j, tt * 512:(tt + 1) * 512], in0=yacc[:, j, tt * 512:(tt + 1) * 512],
                                                                  in1=ps[b][:], op=ALU.add),
                                 reads=[f"ps{b}", f"yacc{j}_{tt}"], writes=[f"yacc{j}_{tt}"])
        for tt in range(2):
            rms_stats(lambda k, tt=tt: yacc[:, k, tt * 512:(tt + 1) * 512], KC, 512, lambda k, tt=tt: [f"yacc{k}_{tt}"], 1.0 / D)
            for c in range(KC):
                i = rot("tmp", 2)
                P.op("dve", lambda e, c=c, i=i: e.tensor_tensor(out=tmpf[i][:], in0=yacc[:, c, tt * 512:(tt + 1) * 512], in1=rstd[:], op=ALU.mult),
                     reads=[f"yacc{c}_{tt}", "rstd"], writes=[f"tmpf{i}"])
                P.op("dve", lambda e, c=c, i=i: e.scalar_tensor_tensor(out=xT[:, c, tt * 512:(tt + 1) * 512], in0=tmpf[i][:], scalar=der[:, 3, c:c + 1],
                                                                       in1=xT[:, c, tt * 512:(tt + 1) * 512], op0=ALU.mult, op1=ALU.add),
                     reads=[f"tmpf{i}", "der", f"x{c}"], writes=[f"x{c}"])
        P.barrier()

    emit_unit(0, True, False, True)
    emit_unit(1, False, False, True)
    load_x(xT2_d)
    emit_unit(0, True, True, False)
    emit_unit(1, True, True, False)

    for k in range(KC):
        P.op("sp", lambda e, k=k: e.dma_start(out=outT_d[k * 128:(k + 1) * 128, :], in_=xT[:, k, :]), reads=[f"x{k}"], writes=[f"out{k}"], dma=True)
    P.op("sp", [], reads=["out"] + [f"out{k}" for k in range(KC)])
    P.barrier(engines=("sp",))
    P.emit()
    return nc


def make_consts():
    c = np.zeros((128, NCONST), np.float32)
    c[:, 0:128] = np.eye(128, dtype=np.float32)
    s = np.arange(128)[:, None]
    t = np.arange(128)[None, :]
    c[:, 128:256] = (s <= t).astype(np.float32)
    dp = (t + 128 - s).astype(np.float32)
    c[:, 256:384] = np.where(s > t, dp, 1e6)
    dc = (t - s).astype(np.float32)
    c[:, 384:512] = np.where(s <= t, dc, 1e6)
    c[:, 512:640] = (s <= t).astype(np.float32)
    return c


def col_major(v, n):
    return np.ascontiguousarray(np.asarray(v, np.float32).reshape(n, 128).T)


ATT_PERM = np.concatenate([np.arange(64) + 64 * attn_head_of_slot(c, hf) for c in range(8) for hf in range(2)])


def layer_inputs(i, l, inp):
    w_in = np.array(inp["w_in"][l], np.float32)
    w_in[:, 3080:3080 + 1024] = w_in[:, 3080 + ATT_PERM]
    w_out = np.array(inp["w_out"][l], np.float32)
    w_out[1024:2048, :] = w_out[1024 + ATT_PERM, :]
    g_at = np.asarray(inp["g_attn_out"][l], np.float32)[ATT_PERM]
    sinks = np.asarray(inp["attn_sinks"][l], np.float32)
    vec = np.zeros((128, NVEC), np.float32)
    vec[:, 0:16] = col_major(inp["g_pre_mix"][l], 16)
    vec[:, 16:32] = col_major(inp["g_post_mix"][l], 16)
    vec[:, 32:48] = col_major(inp["g_pre_mlp"][l], 16)
    vec[:, 48:64] = col_major(inp["g_post_mlp"][l], 16)
    cw = np.asarray(inp["conv_w"][l], np.float32)
    for c in range(8):
        for j in range(4):
            vec[:, 64 + c * 4 + j] = cw[j, c * 128:(c + 1) * 128]
    vec[:, 96:104] = col_major(inp["conv_b"][l], 8)
    vec[:, 104:112] = col_major(np.asarray(inp["g_mlstm_head"][l]).reshape(-1), 8)
    vec[:, 112:120] = col_major(g_at, 8)
    for c in range(8):
        vec[0:64, 120 + c] = sinks[attn_head_of_slot(c, 0)]
        vec[64:128, 120 + c] = sinks[attn_head_of_slot(c, 1)]
    vec[:, 128:132] = np.asarray(inp["b_i"][l], np.float32)[None, :]
    vec[:, 132:136] = np.asarray(inp["b_f"][l], np.float32)[None, :]
    return {
        f"w_ada{i}": np.ascontiguousarray(inp["w_ada"][l], dtype=np.float32),
        f"b_ada{i}": col_major(inp["b_ada"][l], 96),
        f"w_in{i}": w_in, f"w_out{i}": w_out,
        f"w_up{i}": np.ascontiguousarray(inp["w_up"][l], dtype=np.float32),
        f"w_down{i}": np.ascontiguousarray(inp["w_down"][l], dtype=np.float32),
        f"vecs{i}": vec,
    }


_NC_CACHE = {}


def get_nc(dbg=False):
    if dbg not in _NC_CACHE:
        _NC_CACHE[dbg] = build_program([0, 1], dbg=dbg)
    return _NC_CACHE[dbg]


def core_inputs(inp, lw, consts, b, h):
    x = inp["x"]
    m = {"xT1": np.ascontiguousarray(np.asarray(x[b, 0:1024, :], np.float32).T),
         "xT2": np.ascontiguousarray(np.asarray(x[b, h * 1024:(h + 1) * 1024, :], np.float32).T),
         "cT": col_major(inp["c"][b], 16), "consts": consts,
         "st_valid": np.full((128, 1), float(h), np.float32)}
    for d in lw:
        m.update(d)
    return m


def kernel(**inp):
    x = np.asarray(inp["x"])
    B = x.shape[0]
    consts = make_consts()
    lw = [layer_inputs(l, l, inp) for l in range(2)]
    nc = get_nc()
    maps = [core_inputs(inp, lw, consts, b, h) for b in range(B) for h in range(2)]
    res = run_bass_kernel_spmd(nc, maps, core_ids=list(range(2 * B))).results
    out = np.empty(x.shape, np.float32)
    for b in range(B):
        for h in range(2):
            out[b, h * 1024:(h + 1) * 1024, :] = res[2 * b + h]["outT"].T
    return out
```

```python
from contextlib import ExitStack
import math
import numpy as np
import concourse.bass as bass
import concourse.mybir as mybir
from concourse.bass_utils import run_bass_kernel_spmd

F32 = mybir.dt.float32
BF16 = mybir.dt.bfloat16
AF = mybir.ActivationFunctionType
ALU = mybir.AluOpType
ENG = ["pe", "act", "dve", "pool", "sp"]
NDMASEM = 24

D = 2048
KC = 16
TOK = 1024
DIN = 4616
DFF = 8192
EPS = 1e-6
NW = 5
NVEC = 136
NCONST = 640
SB_TOP = 229344


class _Rec:
    def __init__(self):
        self.calls = []

    def __getattr__(self, nm):
        def f(*a, **kw):
            self.calls.append((nm, a, kw))
        return f


class Prog:
    def __init__(self, nc):
        self.nc = nc
        self.ops = {e: [] for e in ENG}
        self.cnt = {e: 0 for e in ENG}
        self.seen = {e: {} for e in ENG}
        self.lastw = {}
        self.readers = {}
        self.dma_cnt = [0] * NDMASEM
        self.dma_rr = 0
        self.dma_rr_e = {}
        self.sb_off = 16640
        self.bank_rr = 0
        self.nalloc = 0

    def sb(self, name, shape, dtype, off=None):
        esz = 4 if dtype == F32 else 2
        n = 1
        for s in shape[1:]:
            n *= s
        nbytes = n * esz
        if off is None:
            off = (self.sb_off + 63) // 64 * 64
            self.sb_off = off + nbytes
        assert off + nbytes <= SB_TOP, (name, off, nbytes)
        self.nalloc += 1
        return self.nc.alloc_sbuf_tensor_at(f"{name}_{self.nalloc}", list(shape), dtype, offset=off)

    def op(self, eng, fns, reads=(), writes=(), dma=False):
        waits = {}

        def need(tok):
            if tok is None:
                return
            k, v = tok
            if waits.get(k, 0) < v:
                waits[k] = v

        for key in reads:
            need(self.lastw.get(key))
        for key in writes:
            need(self.lastw.get(key))
            for t in self.readers.get(key, ()):
                need(t)
        wl = []
        for k, v in waits.items():
            if self.seen[eng].get(k, 0) >= v:
                continue
            self.seen[eng][k] = v
            wl.append((k, v))
        if not isinstance(fns, list):
            fns = [fns]
        rec = _Rec()
        for f in fns:
            f(rec)
        fns = rec.calls
        if len(fns) == 0:
            tok = None
        elif dma:
            lo, n = (0, 8) if eng == "pool" else (8, NDMASEM - 8)
            r = self.dma_rr_e.get(eng, 0)
            self.dma_rr_e[eng] = (r + 1) % n
            s = lo + r
            kprev = ("dma", s)
            if self.dma_cnt[s] > 0 and self.seen[eng].get(kprev, 0) < self.dma_cnt[s]:
                self.seen[eng][kprev] = self.dma_cnt[s]
                wl.append((kprev, self.dma_cnt[s]))
            self.dma_cnt[s] += 16
            tok = (("dma", s), self.dma_cnt[s])
        else:
            self.cnt[eng] += 1
            tok = (eng, self.cnt[eng])
        self.ops[eng].append((wl, fns, tok))
        if tok is not None:
            for key in reads:
                self.readers.setdefault(key, []).append(tok)
            for key in writes:
                self.lastw[key] = tok
                self.readers[key] = []
        return tok

    def barrier(self, engines=("pe", "act", "dve", "sp")):
        for e in engines:
            wl = []
            for k in ENG:
                v = self.cnt[k]
                if v > 0 and self.seen[e].get(k, 0) < v:
                    self.seen[e][k] = v
                    wl.append((k, v))
            for s in range(NDMASEM):
                v = self.dma_cnt[s]
                k = ("dma", s)
                if v > 0 and self.seen[e].get(k, 0) < v:
                    self.seen[e][k] = v
                    wl.append((k, v))
            self.ops[e].append((wl, [], None))

    def emit(self):
        nc = self.nc
        with ExitStack() as st:
            sems = {e: st.enter_context(nc.semaphore(f"s_{e}")) for e in ENG}
            dsems = [st.enter_context(nc.semaphore(f"d{i}")) for i in range(NDMASEM)]
            block = st.enter_context(nc.Block())

            def run(name, e):
                for wl, fns, tok in self.ops[name]:
                    for k, v in wl:
                        sem = sems[k] if isinstance(k, str) else dsems[k[1]]
                        e.wait_ge(sem, v)
                    ins = None
                    for (nm, a, kw) in fns:
                        ins = getattr(e, nm)(*a, **kw)
                    if tok is not None:
                        k, v = tok
                        if isinstance(k, str):
                            ins.then_inc(sems[k], 1)
                        else:
                            ins.then_inc(dsems[k[1]], 16)

            @block.tensor
            def _(e):
                run("pe", e)

            @block.scalar
            def _(e):
                run("act", e)

            @block.vector
            def _(e):
                run("dve", e)

            @block.gpsimd
            def _(e):
                run("pool", e)

            @block.sync
            def _(e):
                run("sp", e)


def attn_head_of_slot(c, hf):
    return (c if c < 4 else 8 + (c - 4)) + 4 * hf


def build_program(layers, dbg=False):
    nc = bass.Bass("TRN2", target_bir_lowering=False)
    P = Prog(nc)

    def din(name, shape):
        return nc.dram_tensor(name, list(shape), F32, kind="ExternalInput").ap()

    def dout(name, shape):
        return nc.dram_tensor(name, list(shape), F32, kind="ExternalOutput").ap()

    xT1_d = din("xT1", [D, TOK])
    xT2_d = din("xT2", [D, TOK])
    cT_d = din("cT", [128, KC])
    consts_d = din("consts", [128, NCONST])
    stval_d = din("st_valid", [128, 1])
    W = []
    for i in range(len(layers)):
        W.append(dict(
            w_ada=din(f"w_ada{i}", [D, 6 * D]), b_ada=din(f"b_ada{i}", [128, 96]),
            w_in=din(f"w_in{i}", [D, DIN]), w_out=din(f"w_out{i}", [D, D]),
            w_up=din(f"w_up{i}", [D, DFF]), w_down=din(f"w_down{i}", [DFF, D]),
            vecs=din(f"vecs{i}", [128, NVEC])))
    outT_d = dout("outT", [D, TOK])
    if dbg:
        dbg_mo_d = dout("dbg_mo", [128, 16, TOK])
        dbg_x_d = dout("dbg_x", [D, TOK])
        dbg_h_d = dout("dbg_h", [D, TOK])
        dbg_mod_d = dout("dbg_mod", [128, 96])

    ps = [nc.alloc_psum_tensor(f"ps{i}", [128, 512], F32) for i in range(8)]

    reserved = set()

    def bank():
        while True:
            b = P.bank_rr
            P.bank_rr = (b + 1) % 8
            if b not in reserved:
                return b

    xT = P.sb("xT", [128, KC, TOK], F32)
    consts = P.sb("consts", [128, NCONST], F32)
    ident_f = consts[:, 0:128]
    mask01 = consts[:, 128:256]
    distp = consts[:, 256:384]
    distc = consts[:, 384:512]
    utri = consts[:, 512:640]
    ident_b = P.sb("ident_b", [128, 128], BF16)
    ones_b = P.sb("ones_b", [128, 128], BF16)
    ones_f = P.sb("ones_f", [128, 128], F32)
    vecs = P.sb("vecs", [128, NVEC], F32)
    mod = P.sb("mod", [128, len(layers), 96], F32)
    bada = P.sb("bada", [128, 96], F32)
    bada1 = P.sb("bada1", [128, 96], F32)
    der = P.sb("der", [128, 4, KC], F32)
    cact = P.sb("cact", [128, KC], BF16)
    cT = P.sb("cT", [128, KC], F32)
    sinkexp = P.sb("sinkexp", [128, 8], F32)
    epst = P.sb("epst", [128, 1], F32)
    lnsct = P.sb("lnsct", [128, 1], F32)
    stval = P.sb("stval", [128, 1], F32)
    S_C = [nc.dram_tensor(f"S_C{l}", [128, 4, 257], F32).ap() for l in range(2)]
    S_conv = [nc.dram_tensor(f"S_conv{l}", [128, 8, 3], F32).ap() for l in range(2)]
    S_k = [nc.dram_tensor(f"S_k{l}", [128, 2, 128], F32).ap() for l in range(2)]
    S_v = [nc.dram_tensor(f"S_v{l}", [128, 256], F32).ap() for l in range(2)]
    wr = [P.sb(f"wr{s}", [128, KC, 128], BF16) for s in range(NW)]
    sqr = [P.sb(f"sq{s}", [128, 512], BF16) for s in range(3)]
    rt = P.sb("rt", [128, 512], F32)
    rstd = P.sb("rstd", [128, 512], F32)
    tmpf = [P.sb(f"tmpf{s}", [128, 512], F32) for s in range(2)]
    R0 = (P.sb_off + 63) // 64 * 64
    st = {"wr": 0, "sq": 0, "tmp": 0}

    def rot(name, n):
        i = st[name]
        st[name] = (i + 1) % n
        return i

    mo = P.sb("mo", [128, 16, TOK], BF16, off=R0)
    hT = P.sb("hT", [128, KC, TOK], BF16, off=R0 + 32768)
    RT = R0 + 65536
    ystash = P.sb("ystash", [128, KC, TOK], F32, off=R0 + 32768)
    actT = P.sb("actT", [128, 64, 512], BF16, off=R0)
    y32 = P.sb("y32", [128, KC, 512], F32, off=R0 + 65536)
    hpT = P.sb("hpT", [128, KC, 512], BF16, off=R0 + 65536)
    assert R0 + 65536 + 40960 <= SB_TOP, R0
    hpF = P.sb("hpF", [128, KC, TOK], BF16, off=R0)
    yacc = P.sb("yacc", [128, KC, TOK], F32, off=R0 + 32768)
    actb = P.sb("actb", [128, 4, TOK], BF16, off=R0 + 98304)

    def wload(wap, k0, c0, ncols):
        s = rot("wr", NW)
        src = wap[k0 * 128:(k0 + KC) * 128, c0:c0 + ncols].rearrange("(k p) n -> p k n", p=128)
        P.op("pool", lambda e, s=s: e.dma_start(out=wr[s][:, :, 0:ncols], in_=src), writes=[f"wr{s}"], dma=True)
        return s

    def wload_w(wap, k0, nk, c0, ncols):
        s = rot("wr", NW)
        src = wap[k0 * 128:(k0 + nk) * 128, c0:c0 + ncols].rearrange("(k p) n -> p k n", p=128)
        dst = wr[s][:].rearrange("p k n -> p (k n)").rearrange("p (k n) -> p k n", k=nk)
        P.op("pool", lambda e: e.dma_start(out=dst, in_=src), writes=[f"wr{s}"], dma=True)
        return s, dst

    def rms_stats(src_fn, nchunks, width, srckeys, scale):
        b = bank()
        pend = None
        for k in range(nchunks):
            s = rot("sq", 3)
            P.op("act", lambda e, s=s, k=k: e.activation(out=sqr[s][:, 0:width], in_=src_fn(k), func=AF.Square),
                 reads=srckeys(k), writes=[f"sq{s}"])
            if pend is not None:
                pk, ps_ = pend
                P.op("pe", lambda e, ps_=ps_, pk=pk: e.matmul(ps[b][:, 0:width], ones_b[:], sqr[ps_][:, 0:width],
                                                              start=(pk == 0), stop=False),
                     reads=[f"sq{ps_}"], writes=[f"ps{b}"])
            pend = (k, s)
        pk, ps_ = pend
        P.op("pe", lambda e: e.matmul(ps[b][:, 0:width], ones_b[:], sqr[ps_][:, 0:width], start=(pk == 0), stop=True),
             reads=[f"sq{ps_}"], writes=[f"ps{b}"])
        P.op("act", lambda e: e.activation(out=rt[:, 0:width], in_=ps[b][:, 0:width], func=AF.Sqrt, scale=scale, bias=epst[:, 0:1]),
             reads=[f"ps{b}"], writes=["rt"])
        P.op("dve", lambda e: e.reciprocal(out=rstd[:, 0:width], in_=rt[:, 0:width]), reads=["rt"], writes=["rstd"])

    def prenorm(dst, dstkey, t0, width, gsi, shi_ap):
        rms_stats(lambda k: xT[:, k, t0:t0 + width], KC, width, lambda k: [f"x{k}"], 1.0 / D)
        for k in range(KC):
            i = rot("tmp", 2)
            P.op("dve", lambda e, k=k, i=i: e.scalar_tensor_tensor(out=tmpf[i][:, 0:width], in0=xT[:, k, t0:t0 + width],
                                                                   scalar=der[:, gsi, k:k + 1], in1=rstd[:, 0:width],
                                                                   op0=ALU.mult, op1=ALU.mult),
                 reads=[f"x{k}", "rstd", "der"], writes=[f"tmpf{i}"])
            P.op("act", lambda e, k=k, i=i: e.activation(out=dst(k), in_=tmpf[i][:, 0:width], func=AF.Identity,
                                                         bias=shi_ap(k), scale=1.0),
                 reads=[f"tmpf{i}", f"mod{cur['li']}"], writes=[dstkey(k)])

    def proj_fm(wap, c0, nchunk, rhs_fn, rhskeys, ntt, evac, k0=0, nk=KC):
        for c in range(nchunk):
            s = wload(wap, k0, c0 + c * 128, 128)
            for tt in range(ntt):
                b = bank()
                fns = [(lambda e, k=k: e.matmul(ps[b][:], wr[s][:, k, :], rhs_fn(k, tt), start=(k == 0), stop=(k == nk - 1)))
                       for k in range(nk)]
                P.op("pe", fns, reads=[f"wr{s}"] + rhskeys(tt), writes=[f"ps{b}"])
                evac(c, tt, b)

    P.op("sp", lambda e: e.dma_start(out=consts[:], in_=consts_d), writes=["consts"], dma=True)
    P.op("sp", lambda e: e.dma_start(out=cT[:], in_=cT_d), writes=["cT"], dma=True)
    P.op("sp", lambda e: e.dma_start(out=stval[:], in_=stval_d), writes=["stval"], dma=True)
    def load_x(src):
        for k in range(KC):
            P.op("sp", lambda e, k=k: e.dma_start(out=xT[:, k, :], in_=src[k * 128:(k + 1) * 128, :]), writes=[f"x{k}"], dma=True)
    load_x(xT1_d)
    P.op("dve", lambda e: e.memset(ones_f[:], 1.0), writes=["ones_f"])
    P.op("dve", lambda e: e.memset(ones_b[:], 1.0), writes=["ones_b"])
    P.op("dve", lambda e: e.memset(epst[:], EPS), writes=["epst"])
    P.op("dve", lambda e: e.memset(lnsct[:], math.log(128.0 ** -0.5)), writes=["lnsct"])
    P.op("dve", lambda e: e.tensor_copy(out=ident_b[:], in_=ident_f), reads=["consts"], writes=["ident_b"])
    P.op("act", lambda e: e.activation(out=cact[:], in_=cT[:], func=AF.Silu), reads=["cT"], writes=["cact"])

    cur = {"li": 0}
    ADA_PRE = 16
    ada = {"next": 0, "bank": None}

    def ada_issue(li, n):
        tiles = []
        for _ in range(n):
            j = ada["next"]
            if j >= 96:
                break
            ada["next"] = j + 1
            tiles.append((wload(W[li]["w_ada"], 0, j * 128, 128), j))
        return tiles

    def ada_consume(tiles):
        b = ada["bank"]
        for (s_, j) in tiles:
            fns = [(lambda e, k=k: e.matmul(ps[b][:, j:j + 1], wr[s_][:, k, :], cact[:, k:k + 1],
                                           start=(k == 0), stop=(k == KC - 1))) for k in range(KC)]
            P.op("pe", fns, reads=[f"wr{s_}", "cact"], writes=[f"ps{b}"])

    def ada_finish(li, badat):
        b = ada["bank"]
        P.op("dve", lambda e: e.tensor_tensor(out=mod[:, li, :], in0=ps[b][:, 0:96], in1=badat[:], op=ALU.add),
             reads=[f"ps{b}", "bada"], writes=[f"mod{li}"])
        reserved.discard(b)

    P.op("sp", lambda e: e.dma_start(out=bada[:], in_=W[0]["b_ada"]), writes=["bada"], dma=True)
    P.op("sp", lambda e: e.dma_start(out=bada1[:], in_=W[1]["b_ada"]), writes=["bada"], dma=True)
    ada["bank"] = bank()
    ada["next"] = 0
    for _ in range(96):
        ada_consume(ada_issue(0, 1))
    ada_finish(0, bada)
    ada["bank"] = bank()
    reserved.add(ada["bank"])
    ada["next"] = 0
    for _ in range(ADA_PRE):
        ada_consume(ada_issue(1, 1))

    def emit_unit(li, full, init, save, hide_ada=False):
        Wl = W[li]
        cur["li"] = li
        P.op("sp", lambda e, li=li: e.dma_start(out=vecs[:], in_=W[li]["vecs"]), writes=["vecs"], dma=True)
        g_pre_mix, g_post_mix, g_pre_mlp, g_post_mlp = (vecs[:, 0:16], vecs[:, 16:32], vecs[:, 32:48], vecs[:, 48:64])
        convw = vecs[:, 64:96]
        convb = vecs[:, 96:104]
        g_ml = vecs[:, 104:112]
        g_at = vecs[:, 112:120]
        sinks2 = vecs[:, 120:128]
        gbias = vecs[:, 128:136]
        shift_a, scale_a, gate_a = mod[:, li, 0:16], mod[:, li, 16:32], mod[:, li, 32:48]
        shift_m, scale_m, gate_m = mod[:, li, 48:64], mod[:, li, 64:80], mod[:, li, 80:96]
        P.op("dve", lambda e: e.scalar_tensor_tensor(out=der[:, 0, :], in0=scale_a, scalar=1.0, in1=g_pre_mix, op0=ALU.add, op1=ALU.mult),
             reads=[f"mod{li}", "vecs"], writes=["der"])
        P.op("dve", lambda e: e.tensor_tensor(out=der[:, 1, :], in0=gate_a, in1=g_post_mix, op=ALU.mult), reads=[f"mod{li}", "vecs"], writes=["der"])
        P.op("dve", lambda e: e.scalar_tensor_tensor(out=der[:, 2, :], in0=scale_m, scalar=1.0, in1=g_pre_mlp, op0=ALU.add, op1=ALU.mult),
             reads=[f"mod{li}", "vecs"], writes=["der"])
        P.op("dve", lambda e: e.tensor_tensor(out=der[:, 3, :], in0=gate_m, in1=g_post_mlp, op=ALU.mult), reads=[f"mod{li}", "vecs"], writes=["der"])
        P.op("act", lambda e: e.activation(out=sinkexp[:], in_=sinks2, func=AF.Exp), reads=["vecs"], writes=["sinkexp"])

        P.barrier()
        for tt in range(2):
            prenorm(lambda k, tt=tt: hT[:, k, tt * 512:(tt + 1) * 512], lambda k, tt=tt: f"hT{tt}", tt * 512, 512, 0,
                    lambda k: shift_a[:, k:k + 1])
        hkeys = lambda tt: [f"hT{tt}"]
        if dbg and li == 0 and not init:
            P.op("sp", lambda e: e.dma_start(out=dbg_mod_d, in_=mod[:, 0, :]), reads=["mod0"], writes=["out"], dma=True)
            for k in range(KC):
                for tt in range(2):
                    i = rot("tmp", 2)
                    P.op("act", lambda e, k=k, tt=tt, i=i: e.activation(out=tmpf[i][:], in_=hT[:, k, tt * 512:(tt + 1) * 512], func=AF.Copy),
                         reads=[f"hT{tt}"], writes=[f"tmpf{i}"])
                    P.op("sp", lambda e, k=k, tt=tt, i=i: e.dma_start(out=dbg_h_d[k * 128:(k + 1) * 128, tt * 512:(tt + 1) * 512], in_=tmpf[i][:]),
                         reads=[f"tmpf{i}"], writes=["out"], dma=True)
        hrhs = lambda k, tt: hT[:, k, tt * 512:(tt + 1) * 512]

        P.sb_off = RT
        q_m = P.sb("q_m", [128, 4, TOK], BF16)
        k_m = P.sb("k_m", [128, 4, TOK], BF16)
        gpre = P.sb("gpre", [128, 8, 8], F32)
        gz = P.sb("gz", [128, 8, 8], F32)
        lf = P.sb("lf", [128, 8, 4], F32)
        nb = P.sb("nb", [128, 8, 4], F32)
        app = P.sb("app", [128, 8, 4], F32)
        eka = P.sb("eka", [128, 8, 4], F32)
        ebend = P.sb("ebend", [128, 8, 4], F32)
        C32 = P.sb("C32", [128, 4, 257], F32)
        C_b = P.sb("C_b", [128, 4, 256], BF16)
        nrep = P.sb("nrep", [128, 4, 128], BF16)
        hraw = P.sb("hraw", [128, 8, 128], F32)
        sq8 = P.sb("sq8", [128, 8, 128], BF16)
        convt = P.sb("convt", [128, 8, 3], F32)
        qtail = P.sb("qtail", [128, 4, 8], F32)
        m_al = P.sb_off
        pre = [P.sb(f"pre{i}", [128, 3 + TOK], F32) for i in range(2)]
        P.sb_off = m_al
        Brep = [P.sb(f"Brep{i}", [128, 128], F32) for i in range(4)]
        Ebt = P.sb("Ebt", [128, 4, 128], F32)
        Mm = P.sb("Mm", [128, 4, 128], F32)
        PT = P.sb("PT", [128, 4, 128], BF16)
        qs = P.sb("qs", [128, 4, 128], BF16)
        kw = P.sb("kw", [128, 4, 128], BF16)
        assert P.sb_off <= RT + 40960 + 2048, P.sb_off - RT
        v_m = P.sb("v_m", [128, 8, 4, 256], BF16, off=R0 + 16384)

        if init:
            P.op("sp", lambda e: e.dma_start(out=convt[:], in_=S_conv[li]), reads=[f"Sconv{li}"], writes=["convt"], dma=True)
            P.op("sp", lambda e: e.dma_start(out=C32[:], in_=S_C[li]), reads=[f"SC{li}"], writes=["C32"], dma=True)
            P.op("dve", lambda e: e.tensor_scalar(out=convt[:], in0=convt[:], scalar1=stval[:, 0:1], scalar2=None, op0=ALU.mult),
                 reads=["convt", "stval"], writes=["convt"])
            P.op("dve", lambda e: e.tensor_scalar(out=C32[:], in0=C32[:], scalar1=stval[:, 0:1], scalar2=None, op0=ALU.mult),
                 reads=["C32", "stval"], writes=["C32"])
        else:
            P.op("dve", lambda e: e.memset(convt[:], 0.0), writes=["convt"])
            P.op("dve", lambda e: e.memset(C32[:], 0.0), writes=["C32"])

        def evac_qk(c, tt, b):
            i = c % 2
            P.op("act", lambda e: e.activation(out=pre[i][:, 3 + tt * 512:3 + (tt + 1) * 512], in_=ps[b][:], func=AF.Copy),
                 reads=[f"ps{b}"], writes=[f"pre{i}"])
            if tt == 1:
                P.op("dve", lambda e: e.tensor_copy(out=pre[i][:, 0:3], in_=convt[:, c, :]), reads=["convt"], writes=[f"pre{i}"])
                P.op("dve", lambda e: e.tensor_scalar(out=cacc[:], in0=pre[i][:, 3:3 + TOK], scalar1=convw[:, c * 4 + 3:c * 4 + 4],
                                                      scalar2=convb[:, c:c + 1], op0=ALU.mult, op1=ALU.add),
                     reads=[f"pre{i}", "vecs"], writes=["cacc"])
                for j in (2, 1, 0):
                    P.op("dve", lambda e, j=j: e.scalar_tensor_tensor(out=cacc[:], in0=pre[i][:, j:j + TOK],
                                                                      scalar=convw[:, c * 4 + j:c * 4 + j + 1], in1=cacc[:],
                                                                      op0=ALU.mult, op1=ALU.add),
                         reads=[f"pre{i}", "vecs", "cacc"], writes=["cacc"])
                dst = q_m[:, c, :] if c < 4 else k_m[:, c - 4, :]
                P.op("act", lambda e: e.activation(out=dst, in_=cacc[:], func=AF.Silu), reads=["cacc"], writes=["qk_m"])
                if save:
                    P.op("sp", lambda e: e.dma_start(out=S_conv[li][:, c, :], in_=pre[i][:, TOK:TOK + 3]), reads=[f"pre{i}"],
                         writes=[f"Sconv{li}"], dma=True)

        cacc = P.sb("cacc", [128, TOK], F32, off=R0)
        if full:
            proj_fm(Wl["w_in"], 0, 8, hrhs, hkeys, 2, evac_qk)
        else:
            for c in range(4):
                s_ = wload(Wl["w_in"], 0, c * 128, 128)
                b_ = bank()
                P.op("pe", [(lambda e, k=k: e.matmul(ps[b_][:, 0:8], wr[s_][:, k, :], hT[:, k, TOK - 8:TOK], start=(k == 0), stop=(k == KC - 1)))
                            for k in range(KC)], reads=[f"wr{s_}", "hT1"], writes=[f"ps{b_}"])
                P.op("act", lambda e: e.activation(out=qtail[:, c, 0:3], in_=ps[b_][:, 5:8], func=AF.Copy), reads=[f"ps{b_}"], writes=["qtail"])
                P.op("sp", lambda e: e.dma_start(out=S_conv[li][:, c, :], in_=qtail[:, c, 0:3]), reads=["qtail"], writes=[f"Sconv{li}"], dma=True)
            proj_fm(Wl["w_in"], 512, 4, hrhs, hkeys, 2, lambda c, tt, b: evac_qk(c + 4, tt, b))

        for cg in range(8):
            s = wload(Wl["w_in"], 0, 1024 + cg * 128, 128)
            for half in range(2):
                b = bank()
                fns = []
                for t4 in range(4):
                    tc = half * 4 + t4
                    fns += [(lambda e, k=k, tc=tc, t4=t4: e.matmul(ps[b][:, t4 * 128:(t4 + 1) * 128], hT[:, k, tc * 128:(tc + 1) * 128],
                                                                   wr[s][:, k, :], start=(k == 0), stop=(k == KC - 1)))
                            for k in range(KC)]
                P.op("pe", fns, reads=[f"wr{s}", "hT0", "hT1"], writes=[f"ps{b}"])
                hh, jj = cg // 2, cg % 2
                P.op("act", lambda e, b=b, half=half, hh=hh, jj=jj: e.activation(
                    out=v_m[:, half * 4:(half + 1) * 4, hh, jj * 128:(jj + 1) * 128],
                    in_=ps[b][:].rearrange("p (t c) -> p t c", t=4), func=AF.Copy),
                     reads=[f"ps{b}"], writes=["v_m"])
        s = wload(Wl["w_in"], 0, 3072, 8)
        b = bank()
        fns = []
        for tc in range(8):
            fns += [(lambda e, k=k, tc=tc: e.matmul(ps[b][:, tc * 8:(tc + 1) * 8], hT[:, k, tc * 128:(tc + 1) * 128],
                                                    wr[s][:, k, 0:8], start=(k == 0), stop=(k == KC - 1))) for k in range(KC)]
        P.op("pe", fns, reads=[f"wr{s}", "hT0", "hT1"], writes=[f"ps{b}"])
        P.op("dve", lambda e, b=b: e.tensor_tensor(out=gz[:], in0=ps[b][:, 0:64].rearrange("p (t c) -> p t c", t=8),
                                                   in1=gbias.unsqueeze(1).broadcast_to([128, 8, 8]), op=ALU.add),
             reads=[f"ps{b}", "vecs"], writes=["gz"])
        P.op("act", lambda e: e.activation(out=gpre[:, :, 4:8], in_=gz[:, :, 4:8], func=AF.Exp, scale=-1.0), reads=["gz"], writes=["gpre"])
        P.op("act", lambda e: e.activation(out=lf[:], in_=gpre[:, :, 4:8], func=AF.Ln, bias=1.0), reads=["gpre"], writes=["lf"])
        b = bank()
        P.op("pe", [lambda e: e.matmul(ps[b][:, 0:32], utri, lf[:].rearrange("p t h -> p (t h)"), start=True, stop=True),
                    lambda e: e.matmul(ps[b][:, 32:64], ones_f[:], lf[:].rearrange("p t h -> p (t h)"), start=True, stop=True)],
             reads=["lf", "consts", "ones_f"], writes=[f"ps{b}"])
        bpos = ps[b][:, 0:32].rearrange("p (t h) -> p t h", t=8)
        bend = ps[b][:, 32:64].rearrange("p (t h) -> p t h", t=8)
        P.op("dve", lambda e: e.tensor_scalar(out=nb[:], in0=bpos, scalar1=-1.0, scalar2=None, op0=ALU.mult), reads=[f"ps{b}"], writes=["nb"])
        P.op("dve", lambda e: e.scalar_tensor_tensor(out=app[:], in0=bpos, scalar=lnsct[:, 0:1], in1=gz[:, :, 0:4], op0=ALU.add, op1=ALU.add),
             reads=[f"ps{b}", "gz", "lnsct"], writes=["app"])
        P.op("dve", lambda e: e.tensor_tensor(out=eka[:], in0=gz[:, :, 0:4], in1=bend, op=ALU.subtract), reads=[f"ps{b}", "gz"], writes=["eka"])
        P.op("dve", lambda e: e.tensor_tensor(out=eka[:], in0=eka[:], in1=nb[:], op=ALU.subtract), reads=["eka", "nb"], writes=["eka"])
        P.op("act", lambda e: e.activation(out=eka[:], in_=eka[:], func=AF.Exp), reads=["eka"], writes=["eka"])
        P.op("act", lambda e: e.activation(out=ebend[:], in_=bend, func=AF.Exp, scale=-1.0), reads=[f"ps{b}"], writes=["ebend"])
        def refresh_state():
            P.op("act", lambda e: e.activation(out=C_b[:], in_=C32[:, :, 0:256], func=AF.Copy), reads=["C32"], writes=["C_b"])
            for h in range(4):
                P.op("act", lambda e, h=h: e.activation(out=nrep[:, h, :], in_=ones_f[:], func=AF.Identity, scale=C32[:, h, 256:257]),
                     reads=["C32", "ones_f"], writes=["nrep"])
        if full:
            refresh_state()
        P.barrier()

        banks_ = {}

        def st_A(tc):
            tsl = slice(tc * 128, (tc + 1) * 128)
            if full:
                bB = bank()
                for h in range(4):
                    P.op("dve", lambda e, h=h: e.tensor_scalar(out=Brep[h][:], in0=ones_f[:], scalar1=nb[:, tc, h:h + 1], scalar2=None, op0=ALU.mult),
                         reads=["nb", "ones_f"], writes=[f"Brep{h}"])
                P.op("pe", [(lambda e, h=h: e.matmul(ps[bB][:, h * 128:(h + 1) * 128], Brep[h][:], ident_f, start=True, stop=True)) for h in range(4)],
                     reads=[f"Brep{h}" for h in range(4)] + ["consts"], writes=[f"ps{bB}"])
                P.op("act", lambda e: e.activation(out=Ebt[:].rearrange("p h t -> p (h t)"), in_=ps[bB][:], func=AF.Exp, bias=lnsct[:, 0:1]),
                     reads=[f"ps{bB}", "lnsct"], writes=["Ebt"])
                for h in range(4):
                    P.op("act", lambda e, h=h: e.activation(out=Mm[:, h, :], in_=ps[bB][:, h * 128:(h + 1) * 128], func=AF.Exp,
                                                            bias=app[:, tc, h:h + 1]), reads=[f"ps{bB}", "app"], writes=["Mm"])
                P.op("dve", lambda e: e.tensor_tensor(out=Mm[:], in0=Mm[:], in1=mask01.unsqueeze(1).broadcast_to([128, 4, 128]), op=ALU.mult),
                     reads=["Mm", "consts"], writes=["Mm"])
                bS = bank()
                P.op("pe", [(lambda e, h=h: e.matmul(ps[bS][:, h * 128:(h + 1) * 128], k_m[:, h, tsl], q_m[:, h, tsl], start=True, stop=True))
                            for h in range(4)], reads=["qk_m"], writes=[f"ps{bS}"])
                P.op("dve", lambda e: e.tensor_tensor(out=PT[:].rearrange("p h t -> p (h t)"), in0=ps[bS][:], in1=Mm[:].rearrange("p h t -> p (h t)"), op=ALU.mult),
                     reads=[f"ps{bS}", "Mm"], writes=["PT"])
                P.op("dve", lambda e: e.tensor_tensor(out=qs[:], in0=q_m[:, :, tsl], in1=Ebt[:], op=ALU.mult), reads=["qk_m", "Ebt"], writes=["qs"])
            bT = bank()
            psT = ps[bT][:].bitcast(BF16)
            P.op("pe", [(lambda e, h=h: e.transpose(psT[:, h * 128:(h + 1) * 128], k_m[:, h, tsl], ident_b[:])) for h in range(4)],
                 reads=["qk_m", "ident_b"], writes=[f"ps{bT}"])
            P.op("dve", lambda e: e.tensor_tensor(out=kw[:], in0=psT[:, 0:512].rearrange("p (h d) -> p h d", h=4),
                                                  in1=eka[:, tc, :].unsqueeze(2).broadcast_to([128, 4, 128]), op=ALU.mult),
                 reads=[f"ps{bT}", "eka"], writes=["kw"])

        def st_Bmm(tc):
            if full:
                bN = [bank(), bank()]
                for hp in range(2):
                    fns = []
                    for h2 in range(2):
                        h = hp * 2 + h2
                        for j in range(2):
                            o_ = ps[bN[hp]][:, (h2 * 2 + j) * 128:(h2 * 2 + j + 1) * 128]
                            fns.append(lambda e, o_=o_, h=h, j=j: e.matmul(o_, v_m[:, tc, h, j * 128:(j + 1) * 128], PT[:, h, :], start=True, stop=False))
                            fns.append(lambda e, o_=o_, h=h, j=j: e.matmul(o_, C_b[:, h, j * 128:(j + 1) * 128], qs[:, h, :], start=False, stop=True))
                    P.op("pe", fns, reads=["v_m", "PT", "C_b", "qs"], writes=[f"ps{bN[hp]}"])
                bD = bank()
                fns = []
                for h in range(4):
                    o_ = ps[bD][:, h * 128:(h + 1) * 128]
                    fns.append(lambda e, o_=o_, h=h: e.matmul(o_, ones_b[:], PT[:, h, :], start=True, stop=False))
                    fns.append(lambda e, o_=o_, h=h: e.matmul(o_, nrep[:, h, :], qs[:, h, :], start=False, stop=True))
                P.op("pe", fns, reads=["PT", "nrep", "qs", "ones_b"], writes=[f"ps{bD}"])
                banks_[("N", tc)] = bN
                banks_[("D", tc)] = bD
                reserved.update([bN[0], bN[1], bD])
            bC = [bank(), bank()]
            bn = bank()
            for hp in range(2):
                P.op("pe", [(lambda e, h2=h2: e.matmul(ps[bC[hp]][:, h2 * 256:(h2 + 1) * 256], kw[:, hp * 2 + h2, :], v_m[:, tc, hp * 2 + h2, :],
                                                       start=True, stop=True)) for h2 in range(2)], reads=["kw", "v_m"], writes=[f"ps{bC[hp]}"])
            P.op("pe", [(lambda e, h=h: e.matmul(ps[bn][:, h:h + 1], kw[:, h, :], ones_b[:, 0:1], start=True, stop=True)) for h in range(4)],
                 reads=["kw", "ones_b"], writes=[f"ps{bn}"])
            banks_[("C", tc)] = (bC, bn)

        def st_Supd(tc):
            bC, bn = banks_[("C", tc)]
            for h in range(4):
                P.op("dve", lambda e, h=h: e.scalar_tensor_tensor(out=C32[:, h, 0:256], in0=C32[:, h, 0:256], scalar=ebend[:, tc, h:h + 1],
                                                                  in1=ps[bC[h // 2]][:, (h % 2) * 256:(h % 2 + 1) * 256], op0=ALU.mult, op1=ALU.add),
                     reads=["C32", "ebend", f"ps{bC[h // 2]}"], writes=["C32"])
                P.op("dve", lambda e, h=h: e.scalar_tensor_tensor(out=C32[:, h, 256:257], in0=C32[:, h, 256:257], scalar=ebend[:, tc, h:h + 1],
                                                                  in1=ps[bn][:, h:h + 1], op0=ALU.mult, op1=ALU.add),
                     reads=["C32", "ebend", f"ps{bn}"], writes=["C32"])
            if tc < 7 and full:
                refresh_state()

        def st_Bpost(tc):
            if not full:
                return
            tsl = slice(tc * 128, (tc + 1) * 128)
            bN = banks_[("N", tc)]
            bD = banks_[("D", tc)]
            P.op("act", lambda e: e.activation(out=tmpf[0][:], in_=ps[bD][:], func=AF.Abs), reads=[f"ps{bD}"], writes=["tmpf0"])
            P.op("dve", lambda e: e.tensor_scalar_max(out=tmpf[0][:], in0=tmpf[0][:], scalar1=1.0), reads=["tmpf0"], writes=["tmpf0"])
            P.op("dve", lambda e: e.reciprocal(out=tmpf[1][:], in_=tmpf[0][:]), reads=["tmpf0"], writes=["tmpf1"])
            rden = tmpf[1][:].rearrange("p (h t) -> p h t", h=4)
            for hp in range(2):
                P.op("dve", lambda e, hp=hp: e.tensor_tensor(
                    out=hraw[:, hp * 4:(hp + 1) * 4, :].rearrange("p (h j) t -> p h j t", h=2),
                    in0=ps[bN[hp]][:].rearrange("p (h j t) -> p h j t", h=2, j=2),
                    in1=rden[:, hp * 2:(hp + 1) * 2, :].unsqueeze(2).broadcast_to([128, 2, 2, 128]), op=ALU.mult),
                     reads=[f"ps{bN[hp]}", "tmpf1"], writes=["hraw"])
            P.op("act", lambda e: e.activation(out=sq8[:], in_=hraw[:], func=AF.Square), reads=["hraw"], writes=["sq8"])
            bQ = bank()
            fns = []
            for h in range(4):
                for j in range(2):
                    fns.append(lambda e, h=h, j=j: e.matmul(ps[bQ][:, h * 128:(h + 1) * 128], ones_b[:], sq8[:, h * 2 + j, :], start=(j == 0), stop=(j == 1)))
            P.op("pe", fns, reads=["sq8", "ones_b"], writes=[f"ps{bQ}"])
            P.op("act", lambda e: e.activation(out=rt[:], in_=ps[bQ][:], func=AF.Sqrt, scale=1.0 / 256, bias=epst[:, 0:1]), reads=[f"ps{bQ}"], writes=["rt"])
            P.op("dve", lambda e: e.reciprocal(out=rstd[:], in_=rt[:]), reads=["rt"], writes=["rstd"])
            rs4 = rstd[:].rearrange("p (h t) -> p h t", h=4)
            P.op("dve", lambda e: e.tensor_tensor(out=mo[:, 0:8, tsl].rearrange("p (h j) t -> p h j t", h=4),
                                                  in0=hraw[:].rearrange("p (h j) t -> p h j t", h=4),
                                                  in1=rs4.unsqueeze(2).broadcast_to([128, 4, 2, 128]), op=ALU.mult),
                 reads=["hraw", "rstd"], writes=["mo_m"])
            for b_ in (bN[0], bN[1], bD):
                reserved.discard(b_)

        st_A(0)
        for tc in range(8):
            ada_t = ada_issue(1, 5) if hide_ada else []
            st_Bmm(tc)
            st_Supd(tc)
            if tc < 7:
                st_A(tc + 1)
            st_Bpost(tc)
            ada_consume(ada_t)
        if save:
            P.op("sp", lambda e: e.dma_start(out=S_C[li], in_=C32[:]), reads=["C32"], writes=[f"SC{li}"], dma=True)

        P.barrier()
        P.sb_off = RT
        qa = P.sb("qa", [128, 8, TOK], BF16)
        kT = P.sb("kT", [128, 2, 128 + TOK], BF16)
        v_a = P.sb("v_a", [128, 9, 256], BF16)
        Pt = [[P.sb(f"Pt{a}{bq}", [128, 512], BF16) for bq in range(2)] for a in range(2)]
        ha32 = P.sb("ha32", [128, 512], F32)
        den_sb = P.sb("den_sb", [128, 512], F32)
        sqa = P.sb("sqa", [128, 512], BF16)
        vallo = P.sb("vallo", [128, 64], BF16)
        stk = P.sb("stk", [128, 2, 128], F32)
        stv = P.sb("stv", [128, 256], F32)
        assert P.sb_off <= RT + 40960, P.sb_off - RT
        if init:
            P.op("sp", lambda e: e.dma_start(out=stk[:], in_=S_k[li]), reads=[f"Sk{li}"], writes=["stk"], dma=True)
            P.op("sp", lambda e: e.dma_start(out=stv[:], in_=S_v[li]), reads=[f"Sv{li}"], writes=["stv"], dma=True)
            P.op("dve", lambda e: e.tensor_scalar(out=stv[:], in0=stv[:], scalar1=stval[:, 0:1], scalar2=None, op0=ALU.mult),
                 reads=["stv", "stval"], writes=["stv"])
        else:
            P.op("dve", lambda e: e.memset(stk[:], 0.0), writes=["stk"])
            P.op("dve", lambda e: e.memset(stv[:], 0.0), writes=["stv"])
        P.op("dve", lambda e: e.tensor_copy(out=kT[:, :, 0:128], in_=stk[:]), reads=["stk"], writes=["kT"])
        P.op("dve", lambda e: e.tensor_copy(out=v_a[:, 0, :], in_=stv[:]), reads=["stv"], writes=["v_a"])
        if init:
            P.op("dve", lambda e: e.tensor_scalar(out=vallo[:], in0=ones_f[:, 0:64], scalar1=stval[:, 0:1], scalar2=None, op0=ALU.mult),
                 reads=["stval", "ones_f"], writes=["vallo"])
        else:
            P.op("dve", lambda e: e.memset(vallo[:], 0.0), writes=["vallo"])

        def evac_qa(c, tt, b):
            P.op("act", lambda e: e.activation(out=qa[:, c, tt * 512:(tt + 1) * 512], in_=ps[b][:], func=AF.Copy), reads=[f"ps{b}"], writes=["qa"])
        if full:
            proj_fm(Wl["w_in"], 3080, 8, hrhs, hkeys, 2, evac_qa)

        def evac_ka(c, tt, b):
            P.op("act", lambda e: e.activation(out=kT[:, c, 128 + tt * 512:128 + (tt + 1) * 512], in_=ps[b][:], func=AF.Copy), reads=[f"ps{b}"], writes=["kT"])
            if tt == 1:
                if save:
                    P.op("act", lambda e: e.activation(out=stk[:, c, :], in_=ps[b][:, 384:512], func=AF.Copy), reads=[f"ps{b}"], writes=["stk"])
                    P.op("sp", lambda e: e.dma_start(out=S_k[li][:, c, :], in_=stk[:, c, :]), reads=["stk"], writes=[f"Sk{li}"], dma=True)
        proj_fm(Wl["w_in"], 3080 + 1024, 2, hrhs, hkeys, 2, evac_ka)
        for cg in range(2):
            s = wload(Wl["w_in"], 0, 3080 + 1280 + cg * 128, 128)
            for half in range(2):
                b = bank()
                fns = []
                for t4 in range(4):
                    tc = half * 4 + t4
                    fns += [(lambda e, k=k, tc=tc, t4=t4: e.matmul(ps[b][:, t4 * 128:(t4 + 1) * 128], hT[:, k, tc * 128:(tc + 1) * 128],
                                                                   wr[s][:, k, :], start=(k == 0), stop=(k == KC - 1)))
                            for k in range(KC)]
                P.op("pe", fns, reads=[f"wr{s}", "hT0", "hT1"], writes=[f"ps{b}"])
                P.op("act", lambda e, b=b, half=half, cg=cg: e.activation(
                    out=v_a[:, 1 + half * 4:1 + (half + 1) * 4, cg * 128:(cg + 1) * 128],
                    in_=ps[b][:].rearrange("p (t c) -> p t c", t=4), func=AF.Copy), reads=[f"ps{b}"], writes=["v_a"])
                if half == 1 and save:
                    P.op("act", lambda e, b=b, cg=cg: e.activation(out=stv[:, cg * 128:(cg + 1) * 128], in_=ps[b][:, 384:512], func=AF.Copy),
                         reads=[f"ps{b}"], writes=["stv"])
                    P.op("sp", lambda e, cg=cg: e.dma_start(out=S_v[li][:, cg * 128:(cg + 1) * 128], in_=stv[:, cg * 128:(cg + 1) * 128]),
                         reads=["stv"], writes=[f"Sv{li}"], dma=True)

        if not full:
            P.barrier()
            return
        def ogate_group(c, s_):
            for tt in range(2):
                b = bank()
                P.op("pe", [(lambda e, k=k: e.matmul(ps[b][:], wr[s_][:, k, :], hT[:, k, tt * 512:(tt + 1) * 512], start=(k == 0), stop=(k == KC - 1)))
                            for k in range(KC)], reads=[f"wr{s_}", f"hT{tt}"], writes=[f"ps{b}"])
                q_ = rot("sq", 3)
                P.op("act", lambda e: e.activation(out=sqr[q_][:], in_=ps[b][:], func=AF.Sigmoid), reads=[f"ps{b}"], writes=[f"sq{q_}"])
                P.op("dve", lambda e: e.scalar_tensor_tensor(out=mo[:, c, tt * 512:(tt + 1) * 512], in0=sqr[q_][:], scalar=g_ml[:, c:c + 1],
                                                             in1=mo[:, c, tt * 512:(tt + 1) * 512], op0=ALU.mult, op1=ALU.mult),
                     reads=[f"sq{q_}", "mo_m", "vecs"], writes=["mo_m"])
        og_slot = wload(Wl["w_in"], 0, 2048, 128)
        for n in range(8):
            qsl = slice(n * 128, (n + 1) * 128)
            ogate_group(n, og_slot)
            if n < 7:
                og_slot = wload(Wl["w_in"], 0, 2048 + (n + 1) * 128, 128)
            ada_t = ada_issue(1, 4) if hide_ada else []
            bSS = bank()
            reserved.add(bSS)
            for kc in range(2):
                for hf in range(2):
                    kvh = kc * 2 + hf
                    psl = slice(hf * 64, (hf + 1) * 64)
                    for kb in range(2):
                        ksl = slice((n + kb) * 128, (n + kb + 1) * 128)
                        b = bank()
                        P.op("pe", lambda e, b=b, ksl=ksl: e.matmul(ps[b][:], kT[psl, kc, ksl], qa[psl, kc * 4:(kc + 1) * 4, qsl], start=True, stop=True),
                             reads=["kT", "qa"], writes=[f"ps{b}"])
                        dist = distp if kb == 0 else distc
                        for g in range(4):
                            head = attn_head_of_slot(kc * 4 + g, hf)
                            slope = 2.0 ** (-(head + 1) / 2.0)
                            i = rot("tmp", 2) if g == 0 else i
                            P.op("dve", lambda e, b=b, g=g, i=i, slope=slope, dist=dist: e.scalar_tensor_tensor(
                                out=tmpf[i][:, g * 128:(g + 1) * 128], in0=dist, scalar=-8.0 * slope, in1=ps[b][:, g * 128:(g + 1) * 128],
                                op0=ALU.mult, op1=ALU.add), reads=[f"ps{b}", "consts"], writes=[f"tmpf{i}"])
                        P.op("act", lambda e, i=i, hf=hf, kb=kb: e.activation(out=Pt[hf][kb][:], in_=tmpf[i][:], func=AF.Exp, scale=0.125),
                             reads=[f"tmpf{i}"], writes=[f"Pt{hf}{kb}"])
                bNn = bank()
                bDd = bank()
                fns = []
                fnd = []
                for hf in range(2):
                    kvh = kc * 2 + hf
                    for kb in range(2):
                        vsl = slice(kvh * 64, (kvh + 1) * 64)
                        fns.append(lambda e, hf=hf, kb=kb, vsl=vsl: e.matmul(ps[bNn][hf * 64:(hf + 1) * 64, :], v_a[:, n + kb, vsl], Pt[hf][kb][:],
                                                                             start=(kb == 0), stop=(kb == 1), tile_position=(0, hf * 64)))
                        lo = vallo[:] if (n == 0 and kb == 0) else ones_b[:, 0:64]
                        fnd.append(lambda e, hf=hf, kb=kb, lo=lo: e.matmul(ps[bDd][hf * 64:(hf + 1) * 64, :], lo, Pt[hf][kb][:],
                                                                           start=(kb == 0), stop=(kb == 1), tile_position=(0, hf * 64)))
                P.op("pe", fns, reads=["v_a", "Pt00", "Pt01", "Pt10", "Pt11"], writes=[f"ps{bNn}"])
                P.op("pe", fnd, reads=["vallo", "ones_b", "Pt00", "Pt01", "Pt10", "Pt11"], writes=[f"ps{bDd}"])
                P.op("dve", lambda e, kc=kc: e.tensor_tensor(out=den_sb[:].rearrange("p (g t) -> p g t", g=4),
                                                             in0=ps[bDd][:].rearrange("p (g t) -> p g t", g=4),
                                                             in1=sinkexp[:, kc * 4:(kc + 1) * 4].unsqueeze(2).broadcast_to([128, 4, 128]), op=ALU.add),
                     reads=[f"ps{bDd}", "sinkexp"], writes=["den_sb"])
                P.op("dve", lambda e: e.reciprocal(out=den_sb[:], in_=den_sb[:]), reads=["den_sb"], writes=["den_sb"])
                P.op("dve", lambda e: e.tensor_tensor(out=ha32[:], in0=ps[bNn][:], in1=den_sb[:], op=ALU.mult), reads=[f"ps{bNn}", "den_sb"], writes=["ha32"])
                P.op("act", lambda e: e.activation(out=sqa[:], in_=ha32[:], func=AF.Square), reads=["ha32"], writes=["sqa"])
                P.op("act", lambda e, kc=kc: e.activation(out=mo[:, 8 + kc * 4:8 + (kc + 1) * 4, qsl], in_=ha32[:].rearrange("p (g t) -> p g t", g=4), func=AF.Copy),
                     reads=["ha32"], writes=["mo_a"])
                P.op("pe", [(lambda e, g=g, kc=kc: e.matmul(ps[bSS][:, 0:128], ones_b[:], sqa[:, g * 128:(g + 1) * 128],
                                                            start=(kc == 0 and g == 0), stop=(kc == 1 and g == 3))) for g in range(4)],
                     reads=["sqa", "ones_b"], writes=[f"ps{bSS}"])
            reserved.discard(bSS)
            P.op("act", lambda e: e.activation(out=rt[:, 0:128], in_=ps[bSS][:, 0:128], func=AF.Sqrt, scale=1.0 / 1024, bias=epst[:, 0:1]),
                 reads=[f"ps{bSS}"], writes=["rt"])
            P.op("dve", lambda e: e.reciprocal(out=rstd[:, 0:128], in_=rt[:, 0:128]), reads=["rt"], writes=["rstd"])
            for c in range(8):
                P.op("dve", lambda e, c=c: e.scalar_tensor_tensor(out=mo[:, 8 + c, qsl], in0=mo[:, 8 + c, qsl], scalar=g_at[:, c:c + 1],
                                                                  in1=rstd[:, 0:128], op0=ALU.mult, op1=ALU.mult),
                     reads=["mo_a", "rstd", "vecs"], writes=["mo_a"])
            ada_consume(ada_t)
        if hide_ada:
            while ada["next"] < 96:
                ada_consume(ada_issue(1, 1))
            ada_finish(1, bada1)

        P.barrier()
        if dbg and li == 0 and not init:
            for c in range(16):
                for tt in range(2):
                    i = rot("tmp", 2)
                    P.op("act", lambda e, c=c, tt=tt, i=i: e.activation(out=tmpf[i][:], in_=mo[:, c, tt * 512:(tt + 1) * 512], func=AF.Copy),
                         reads=["mo_m", "mo_a"], writes=[f"tmpf{i}"])
                    P.op("sp", lambda e, c=c, tt=tt, i=i: e.dma_start(out=dbg_mo_d[:, c, tt * 512:(tt + 1) * 512], in_=tmpf[i][:]),
                         reads=[f"tmpf{i}"], writes=["out"], dma=True)
        pend = []

        def flush_ss(final=False):
            while pend and (final or len(pend) > 1):
                c, tt, s, bss = pend.pop(0)
                P.op("pe", lambda e, c=c, s=s, bss=bss: e.matmul(ps[bss][:], ones_b[:], sqr[s][:], start=(c == 0), stop=(c == KC - 1)),
                     reads=[f"sq{s}"], writes=[f"ps{bss}"])

        bss2 = [bank(), bank()]
        reserved.update(bss2)

        def evac_y(c, tt, b):
            P.op("act", lambda e: e.activation(out=ystash[:, c, tt * 512:(tt + 1) * 512], in_=ps[b][:], func=AF.Copy), reads=[f"ps{b}"], writes=[f"ys{tt}"])
            s = rot("sq", 3)
            P.op("act", lambda e: e.activation(out=sqr[s][:], in_=ps[b][:], func=AF.Square), reads=[f"ps{b}"], writes=[f"sq{s}"])
            pend.append((c, tt, s, bss2[tt]))
            flush_ss()
        proj_fm(Wl["w_out"], 0, KC, lambda k, tt: mo[:, k, tt * 512:(tt + 1) * 512], lambda tt: ["mo_m", "mo_a"], 2, evac_y)
        flush_ss(final=True)
        reserved.discard(bss2[0]); reserved.discard(bss2[1])

        def residual(tt, bss, src_fn, srckey, ggi, t0):
            P.op("act", lambda e: e.activation(out=rt[:], in_=ps[bss][:], func=AF.Sqrt, scale=1.0 / D, bias=epst[:, 0:1]), reads=[f"ps{bss}"], writes=["rt"])
            P.op("dve", lambda e: e.reciprocal(out=rstd[:], in_=rt[:]), reads=["rt"], writes=["rstd"])
            for c in range(KC):
                i = rot("tmp", 2)
                P.op("dve", lambda e, c=c, i=i: e.tensor_tensor(out=tmpf[i][:], in0=src_fn(c), in1=rstd[:], op=ALU.mult),
                     reads=[srckey, "rstd"], writes=[f"tmpf{i}"])
                P.op("dve", lambda e, c=c, i=i: e.scalar_tensor_tensor(out=xT[:, c, t0:t0 + 512], in0=tmpf[i][:], scalar=der[:, ggi, c:c + 1],
                                                                       in1=xT[:, c, t0:t0 + 512], op0=ALU.mult, op1=ALU.add),
                     reads=[f"tmpf{i}", "der", f"x{c}"], writes=[f"x{c}"])
        for tt in range(2):
            residual(tt, bss2[tt], lambda c, tt=tt: ystash[:, c, tt * 512:(tt + 1) * 512], f"ys{tt}", 1, tt * 512)

        P.barrier()
        if dbg and li == 0 and not init:
            for k in range(KC):
                P.op("sp", lambda e, k=k: e.dma_start(out=dbg_x_d[k * 128:(k + 1) * 128, :], in_=xT[:, k, :]), reads=[f"x{k}"], writes=["out"], dma=True)
        for tt in range(2):
            prenorm(lambda k, tt=tt: hpF[:, k, tt * 512:(tt + 1) * 512], lambda k, tt=tt: f"hpF{tt}", tt * 512, 512, 2,
                    lambda k: shift_m[:, k:k + 1])
        NFB = 16
        for fb in range(NFB):
            for c4 in range(4):
                s_ = wload(Wl["w_up"], 0, (fb * 4 + c4) * 128, 128)
                for tt in range(2):
                    b = bank()
                    P.op("pe", [(lambda e, k=k: e.matmul(ps[b][:], wr[s_][:, k, :], hpF[:, k, tt * 512:(tt + 1) * 512], start=(k == 0), stop=(k == KC - 1)))
                                for k in range(KC)], reads=[f"wr{s_}", f"hpF{tt}"], writes=[f"ps{b}"])
                    i = rot("tmp", 2)
                    P.op("act", lambda e: e.activation(out=tmpf[i][:], in_=ps[b][:], func=AF.Relu), reads=[f"ps{b}"], writes=[f"tmpf{i}"])
                    P.op("dve", lambda e: e.tensor_tensor(out=actb[:, c4, tt * 512:(tt + 1) * 512], in0=tmpf[i][:], in1=tmpf[i][:], op=ALU.mult),
                         reads=[f"tmpf{i}"], writes=["actb"])
            for quarter in range(4):
                s_, wv = wload_w(Wl["w_down"], fb * 4, 4, quarter * 512, 512)
                for j4 in range(4):
                    j = quarter * 4 + j4
                    for tt in range(2):
                        b = bank()
                        P.op("pe", [(lambda e, c4=c4: e.matmul(ps[b][:], wv[:, c4, j4 * 128:(j4 + 1) * 128], actb[:, c4, tt * 512:(tt + 1) * 512],
                                                               start=(c4 == 0), stop=(c4 == 3))) for c4 in range(4)],
                             reads=[f"wr{s_}", "actb"], writes=[f"ps{b}"])
                        if fb == 0:
                            P.op("act", lambda e: e.activation(out=yacc[:, j, tt * 512:(tt + 1) * 512], in_=ps[b][:], func=AF.Copy),
                                 reads=[f"ps{b}"], writes=[f"yacc{j}_{tt}"])
                        else:
                            P.op("dve", lambda e: e.tensor_tensor(out=yacc[:, j, tt * 512:(tt + 1) * 512], in0=yacc[:, j, tt * 512:(tt + 1) * 512],
                                                                  in1=ps[b][:], op=ALU.add),
                                 reads=[f"ps{b}", f"yacc{j}_{tt}"], writes=[f"yacc{j}_{tt}"])
        for tt in range(2):
            rms_stats(lambda k, tt=tt: yacc[:, k, tt * 512:(tt + 1) * 512], KC, 512, lambda k, tt=tt: [f"yacc{k}_{tt}"], 1.0 / D)
            for c in range(KC):
                i = rot("tmp", 2)
                P.op("dve", lambda e, c=c, i=i: e.tensor_tensor(out=tmpf[i][:], in0=yacc[:, c, tt * 512:(tt + 1) * 512], in1=rstd[:], op=ALU.mult),
                     reads=[f"yacc{c}_{tt}", "rstd"], writes=[f"tmpf{i}"])
                P.op("dve", lambda e, c=c, i=i: e.scalar_tensor_tensor(out=xT[:, c, tt * 512:(tt + 1) * 512], in0=tmpf[i][:], scalar=der[:, 3, c:c + 1],
                                                                       in1=xT[:, c, tt * 512:(tt + 1) * 512], op0=ALU.mult, op1=ALU.add),
                     reads=[f"tmpf{i}", "der", f"x{c}"], writes=[f"x{c}"])
        P.barrier()

    emit_unit(0, True, False, True, hide_ada=True)
    emit_unit(1, False, False, True)
    load_x(xT2_d)
    emit_unit(0, True, True, False)
    emit_unit(1, True, True, False)

    for k in range(KC):
        P.op("sp", lambda e, k=k: e.dma_start(out=outT_d[k * 128:(k + 1) * 128, :], in_=xT[:, k, :]), reads=[f"x{k}"], writes=[f"out{k}"], dma=True)
    P.op("sp", [], reads=["out"] + [f"out{k}" for k in range(KC)])
    P.barrier(engines=("sp",))
    P.emit()
    return nc


def make_consts():
    c = np.zeros((128, NCONST), np.float32)
    c[:, 0:128] = np.eye(128, dtype=np.float32)
    s = np.arange(128)[:, None]
    t = np.arange(128)[None, :]
    c[:, 128:256] = (s <= t).astype(np.float32)
    dp = (t + 128 - s).astype(np.float32)
    c[:, 256:384] = np.where(s > t, dp, 1e6)
    dc = (t - s).astype(np.float32)
    c[:, 384:512] = np.where(s <= t, dc, 1e6)
    c[:, 512:640] = (s <= t).astype(np.float32)
    return c


def col_major(v, n):
    return np.ascontiguousarray(np.asarray(v, np.float32).reshape(n, 128).T)


ATT_PERM = np.concatenate([np.arange(64) + 64 * attn_head_of_slot(c, hf) for c in range(8) for hf in range(2)])


def layer_inputs(i, l, inp):
    w_in = np.array(inp["w_in"][l], np.float32)
    w_in[:, 3080:3080 + 1024] = w_in[:, 3080 + ATT_PERM]
    w_out = np.array(inp["w_out"][l], np.float32)
    w_out[1024:2048, :] = w_out[1024 + ATT_PERM, :]
    g_at = np.asarray(inp["g_attn_out"][l], np.float32)[ATT_PERM]
    sinks = np.asarray(inp["attn_sinks"][l], np.float32)
    vec = np.zeros((128, NVEC), np.float32)
    vec[:, 0:16] = col_major(inp["g_pre_mix"][l], 16)
    vec[:, 16:32] = col_major(inp["g_post_mix"][l], 16)
    vec[:, 32:48] = col_major(inp["g_pre_mlp"][l], 16)
    vec[:, 48:64] = col_major(inp["g_post_mlp"][l], 16)
    cw = np.asarray(inp["conv_w"][l], np.float32)
    for c in range(8):
        for j in range(4):
            vec[:, 64 + c * 4 + j] = cw[j, c * 128:(c + 1) * 128]
    vec[:, 96:104] = col_major(inp["conv_b"][l], 8)
    vec[:, 104:112] = col_major(np.asarray(inp["g_mlstm_head"][l]).reshape(-1), 8)
    vec[:, 112:120] = col_major(g_at, 8)
    for c in range(8):
        vec[0:64, 120 + c] = sinks[attn_head_of_slot(c, 0)]
        vec[64:128, 120 + c] = sinks[attn_head_of_slot(c, 1)]
    vec[:, 128:132] = np.asarray(inp["b_i"][l], np.float32)[None, :]
    vec[:, 132:136] = np.asarray(inp["b_f"][l], np.float32)[None, :]
    return {
        f"w_ada{i}": np.ascontiguousarray(inp["w_ada"][l], dtype=np.float32),
        f"b_ada{i}": col_major(inp["b_ada"][l], 96),
        f"w_in{i}": w_in, f"w_out{i}": w_out,
        f"w_up{i}": np.ascontiguousarray(inp["w_up"][l], dtype=np.float32),
        f"w_down{i}": np.ascontiguousarray(inp["w_down"][l], dtype=np.float32),
        f"vecs{i}": vec,
    }


_NC_CACHE = {}


def get_nc(dbg=False):
    if dbg not in _NC_CACHE:
        _NC_CACHE[dbg] = build_program([0, 1], dbg=dbg)
    return _NC_CACHE[dbg]


def core_inputs(inp, lw, consts, b, h):
    x = inp["x"]
    m = {"xT1": np.ascontiguousarray(np.asarray(x[b, 0:1024, :], np.float32).T),
         "xT2": np.ascontiguousarray(np.asarray(x[b, h * 1024:(h + 1) * 1024, :], np.float32).T),
         "cT": col_major(inp["c"][b], 16), "consts": consts,
         "st_valid": np.full((128, 1), float(h), np.float32)}
    for d in lw:
        m.update(d)
    return m


def kernel(**inp):
    x = np.asarray(inp["x"])
    B = x.shape[0]
    consts = make_consts()
    lw = [layer_inputs(l, l, inp) for l in range(2)]
    nc = get_nc()
    maps = [core_inputs(inp, lw, consts, b, h) for b in range(B) for h in range(2)]
    res = run_bass_kernel_spmd(nc, maps, core_ids=list(range(2 * B))).results
    out = np.empty(x.shape, np.float32)
    for b in range(B):
        for h in range(2):
            out[b, h * 1024:(h + 1) * 1024, :] = res[2 * b + h]["outT"].T
    return out
```
